# Optimizing a Trainium2 kernel written in Bass

```python
import math
import jax, jax.numpy as jnp
from jax import lax
import numpy as np

D_MODEL = 1024
BATCH = 16
SEQ = 2048
DEPTH = 4

N_MIXERS = 3
PLE_DIM = 256
D_FF = 4 * D_MODEL
EPS = 1e-6
NEG_INF = -1e30
S5_GROUP = 16
S5_GROUPS = D_MODEL // S5_GROUP
S5_STATE = 64
S5_CHUNK = 128
FOX_HEADS = 16
FOX_HEAD_DIM = D_MODEL // FOX_HEADS
FOX_IN = 3 * D_MODEL + FOX_HEADS
Q_BLOCK = 128
GLA_HEADS = 4
GLA_KD = D_MODEL // 2
GLA_VD = D_MODEL
GLA_DK = GLA_KD // GLA_HEADS
GLA_DV = GLA_VD // GLA_HEADS
GLA_RANK = 16
GLA_GATE_NORM = 16.0
GLA_CHUNK = 64
GLA_IN = 2 * GLA_KD + 2 * GLA_VD + GLA_RANK
N_A = (DEPTH + 2) // 3
N_B = (DEPTH + 1) // 3
N_C = DEPTH // 3

kernel_name = "hybrid_s5_fox_gla_trunk"


def rmsnorm(x, g):
    xf = x.astype(jnp.float32)
    y = xf * lax.rsqrt(jnp.mean(xf * xf, axis=-1, keepdims=True) + EPS)
    return (y * g.astype(jnp.float32)).astype(x.dtype)


def _cplx_combine(e1, e2):
    a1r, a1i, b1r, b1i = e1
    a2r, a2i, b2r, b2i = e2
    ar = a2r * a1r - a2i * a1i
    ai = a2r * a1i + a2i * a1r
    br = a2r * b1r - a2i * b1i + b2r
    bi = a2r * b1i + a2i * b1r + b2i
    return ar, ai, br, bi


def s5_mixer(xn, w_in, lam_re, lam_im, log_dt, b_re, b_im, c_re, c_im, d_skip, w_glu, w_out):
    bsz, t, _ = xn.shape
    f32 = jnp.float32
    n_chunks = t // S5_CHUNK
    u = (xn @ w_in).astype(f32)
    dt = jnp.exp(log_dt.astype(f32))[:, None]
    lr, li = lam_re.astype(f32), lam_im.astype(f32)
    mag = jnp.exp(lr * dt)
    ab_re, ab_im = mag * jnp.cos(li * dt), mag * jnp.sin(li * dt)
    den = lr * lr + li * li
    nr = ab_re - 1.0
    coef_re = (nr * lr + ab_im * li) / den
    coef_im = (ab_im * lr - nr * li) / den
    br, bi = b_re.astype(f32), b_im.astype(f32)
    bbar_re = coef_re[..., None] * br - coef_im[..., None] * bi
    bbar_im = coef_re[..., None] * bi + coef_im[..., None] * br
    cr, ci = c_re.astype(f32), c_im.astype(f32)
    shape = (bsz, S5_CHUNK, S5_GROUPS, S5_STATE)
    a_re = jnp.broadcast_to(ab_re, shape)
    a_im = jnp.broadcast_to(ab_im, shape)
    uc = u.reshape(bsz, n_chunks, S5_CHUNK, S5_GROUPS, S5_GROUP).swapaxes(0, 1)

    def chunk_step(carry, u_blk):
        h_re, h_im = carry
        x_re = jnp.einsum('bsgc,gnc->bsgn', u_blk, bbar_re)
        x_im = jnp.einsum('bsgc,gnc->bsgn', u_blk, bbar_im)
        pa_re, pa_im, s_re, s_im = lax.associative_scan(
            _cplx_combine, (a_re, a_im, x_re, x_im), axis=1)
        s_re = s_re + pa_re * h_re[:, None] - pa_im * h_im[:, None]
        s_im = s_im + pa_re * h_im[:, None] + pa_im * h_re[:, None]
        y = jnp.einsum('bsgn,gcn->bsgc', s_re, cr) - jnp.einsum('bsgn,gcn->bsgc', s_im, ci)
        return (s_re[:, -1], s_im[:, -1]), y

    h0 = (jnp.zeros((bsz, S5_GROUPS, S5_STATE), f32), jnp.zeros((bsz, S5_GROUPS, S5_STATE), f32))
    _, y = lax.scan(chunk_step, h0, uc)
    y = y.swapaxes(0, 1).reshape(bsz, t, D_MODEL) + d_skip.astype(f32) * u
    z = jax.nn.gelu(y).astype(xn.dtype) @ w_glu
    z1, z2 = jnp.split(z, 2, axis=-1)
    return (z1 * jax.nn.sigmoid(z2)) @ w_out


def fox_mixer(xn, w_in, b_f, w_out):
    bsz, t, _ = xn.shape
    n_blocks = t // Q_BLOCK
    proj = xn @ w_in
    q, k, v = (proj[..., i * D_MODEL:(i + 1) * D_MODEL]
               .reshape(bsz, t, FOX_HEADS, FOX_HEAD_DIM).transpose(0, 2, 1, 3) for i in range(3))
    log_f = jax.nn.log_sigmoid((proj[..., 3 * D_MODEL:] + b_f).astype(jnp.float32))
    cum_f = jnp.cumsum(log_f, axis=1).transpose(0, 2, 1)
    scale = FOX_HEAD_DIM ** -0.5
    qb = q.reshape(bsz, FOX_HEADS, n_blocks, Q_BLOCK, FOX_HEAD_DIM).transpose(2, 0, 1, 3, 4)
    fb = cum_f.reshape(bsz, FOX_HEADS, n_blocks, Q_BLOCK).transpose(2, 0, 1, 3)
    q_pos = jnp.arange(t).reshape(n_blocks, Q_BLOCK)
    k_pos = jnp.arange(t)

    def block(args):
        q_blk, f_blk, pos = args
        s = jnp.einsum('bhqd,bhkd->bhqk', q_blk, k).astype(jnp.float32) * scale
        s = s + (f_blk[..., :, None] - cum_f[:, :, None, :])
        s = jnp.where(pos[:, None] >= k_pos[None, :], s, NEG_INF)
        pr = jax.nn.softmax(s, axis=-1)
        return jnp.einsum('bhqk,bhkd->bhqd', pr.astype(v.dtype), v)

    o = lax.map(block, (qb, fb, q_pos))
    o = o.transpose(1, 0, 3, 2, 4).reshape(bsz, t, D_MODEL)
    return o @ w_out


def gla_mixer(xn, w_in, w_g2, b_g, gn_g, w_out):
    bsz, t, _ = xn.shape
    f32 = jnp.float32
    n_chunks = t // GLA_CHUNK
    proj = xn @ w_in
    q = proj[..., :GLA_KD]
    k = proj[..., GLA_KD:2 * GLA_KD]
    v = proj[..., 2 * GLA_KD:2 * GLA_KD + GLA_VD]
    r = proj[..., 2 * GLA_KD + GLA_VD:2 * GLA_KD + 2 * GLA_VD]
    g_lr = proj[..., 2 * GLA_KD + 2 * GLA_VD:]
    log_a = jax.nn.log_sigmoid((g_lr @ w_g2 + b_g).astype(f32)) / GLA_GATE_NORM

    def to_chunks(z, d):
        return z.reshape(bsz, n_chunks, GLA_CHUNK, GLA_HEADS, d).transpose(1, 0, 3, 2, 4)

    qc = to_chunks(q.astype(f32) * GLA_DK ** -0.5, GLA_DK)
    kc = to_chunks(k.astype(f32), GLA_DK)
    vc = to_chunks(v.astype(f32), GLA_DV)
    gc = to_chunks(log_a, GLA_DK)
    causal = jnp.tril(jnp.ones((GLA_CHUNK, GLA_CHUNK), dtype=bool))

    def chunk_step(state, xs):
        q_blk, k_blk, v_blk, g_blk = xs
        bcum = jnp.cumsum(g_blk, axis=2)
        b_last = bcum[:, :, -1:]
        q_dec = q_blk * jnp.exp(bcum)
        k_inv = k_blk * jnp.exp(-bcum)
        k_upd = k_blk * jnp.exp(b_last - bcum)
        att = jnp.where(causal, jnp.einsum('bhtk,bhsk->bhts', q_dec, k_inv), 0.0)
        o = jnp.einsum('bhts,bhsv->bhtv', att, v_blk) + jnp.einsum('bhtk,bhkv->bhtv', q_dec, state)
        state = jnp.exp(b_last).swapaxes(-1, -2) * state + jnp.einsum('bhsk,bhsv->bhkv', k_upd, v_blk)
        return state, o

    s0 = jnp.zeros((bsz, GLA_HEADS, GLA_DK, GLA_DV), f32)
    _, o = lax.scan(chunk_step, s0, (qc, kc, vc, gc))
    o = o.transpose(1, 0, 3, 2, 4).reshape(bsz, t, GLA_HEADS, GLA_DV)
    o = o * lax.rsqrt(jnp.mean(o * o, axis=-1, keepdims=True) + EPS)
    o = o.reshape(bsz, t, GLA_VD) * gn_g.astype(f32)
    o = (o * jax.nn.silu(r.astype(f32))).astype(xn.dtype)
    return o @ w_out


def sqrelu_mlp(xn, w1, w2):
    return jnp.square(jax.nn.relu(xn @ w1)) @ w2


def setup_inputs(seed: int = 0) -> dict:
    key = jax.random.key(seed)
    ks = iter(jax.random.split(key, 40))
    f32 = jnp.float32

    def nrm(shape, scale):
        return jax.random.normal(next(ks), shape, f32) * scale

    def gain(shape):
        return 1.0 + nrm(shape, 0.02)

    D = D_MODEL
    inputs = {}
    inputs['x'] = nrm((BATCH, SEQ, D), 1.0)
    inputs['p'] = nrm((DEPTH, BATCH, SEQ, PLE_DIM), 1.0)
    inputs['norm_mix'] = gain((DEPTH, D))
    inputs['norm_mlp'] = gain((DEPTH, D))
    inputs['norm_ple'] = gain((DEPTH, D))
    inputs['s5_w_in'] = nrm((N_A, D, D), D ** -0.5)
    inputs['s5_lam_re'] = -0.5 + nrm((N_A, S5_GROUPS, S5_STATE), 0.01)
    inputs['s5_lam_im'] = (math.pi * jnp.arange(S5_STATE, dtype=f32))[None, None, :] + nrm((N_A, S5_GROUPS, S5_STATE), 0.01)
    inputs['s5_log_dt'] = jax.random.uniform(next(ks), (N_A, S5_GROUPS), f32, minval=math.log(1e-3), maxval=math.log(1e-1))
    inputs['s5_b_re'] = nrm((N_A, S5_GROUPS, S5_STATE, S5_GROUP), (2 * S5_GROUP) ** -0.5)
    inputs['s5_b_im'] = nrm((N_A, S5_GROUPS, S5_STATE, S5_GROUP), (2 * S5_GROUP) ** -0.5)
    inputs['s5_c_re'] = nrm((N_A, S5_GROUPS, S5_GROUP, S5_STATE), (S5_STATE) ** -0.5)
    inputs['s5_c_im'] = nrm((N_A, S5_GROUPS, S5_GROUP, S5_STATE), (S5_STATE) ** -0.5)
    inputs['s5_d'] = nrm((N_A, D), 1.0)
    inputs['s5_w_glu'] = nrm((N_A, D, 2 * D), D ** -0.5)
    inputs['s5_w_out'] = nrm((N_A, D, D), D ** -0.5)
    inputs['fox_w_in'] = nrm((N_B, D, FOX_IN), D ** -0.5)
    inputs['fox_b_f'] = jax.random.uniform(next(ks), (N_B, FOX_HEADS), f32, minval=1.0, maxval=5.0)
    inputs['fox_w_out'] = nrm((N_B, D, D), D ** -0.5)
    inputs['gla_w_in'] = nrm((N_C, D, GLA_IN), D ** -0.5)
    inputs['gla_w_g2'] = nrm((N_C, GLA_RANK, GLA_KD), GLA_RANK ** -0.5)
    inputs['gla_b_g'] = nrm((N_C, GLA_KD), 0.1)
    inputs['gla_norm'] = gain((N_C, GLA_VD))
    inputs['gla_w_out'] = nrm((N_C, GLA_VD, D), GLA_VD ** -0.5)
    inputs['mlp_w1'] = nrm((DEPTH, D, D_FF), D ** -0.5)
    inputs['mlp_w2'] = nrm((DEPTH, D_FF, D), D_FF ** -0.5)
    inputs['ple_proj'] = nrm((DEPTH, PLE_DIM, D), PLE_DIM ** -0.5)
    inputs['ple_gate'] = nrm((DEPTH, D, D), D ** -0.5)
    inputs['final_norm'] = gain((D,))
    return inputs


def reference(x, p, norm_mix, norm_mlp, norm_ple,
              s5_w_in, s5_lam_re, s5_lam_im, s5_log_dt, s5_b_re, s5_b_im, s5_c_re, s5_c_im,
              s5_d, s5_w_glu, s5_w_out,
              fox_w_in, fox_b_f, fox_w_out,
              gla_w_in, gla_w_g2, gla_b_g, gla_norm, gla_w_out,
              mlp_w1, mlp_w2, ple_proj, ple_gate, final_norm):
    h = x
    for i in range(DEPTH):
        mixer, j = i % N_MIXERS, i // N_MIXERS
        xn = rmsnorm(h, norm_mix[i])
        if mixer == 0:
            y = s5_mixer(xn, s5_w_in[j], s5_lam_re[j], s5_lam_im[j], s5_log_dt[j], s5_b_re[j], s5_b_im[j],
                         s5_c_re[j], s5_c_im[j], s5_d[j], s5_w_glu[j], s5_w_out[j])
        elif mixer == 1:
            y = fox_mixer(xn, fox_w_in[j], fox_b_f[j], fox_w_out[j])
        else:
            y = gla_mixer(xn, gla_w_in[j], gla_w_g2[j], gla_b_g[j], gla_norm[j], gla_w_out[j])
        h = h + y.astype(h.dtype)
        h = h + sqrelu_mlp(rmsnorm(h, norm_mlp[i]), mlp_w1[i], mlp_w2[i])
        gate = jax.nn.sigmoid(rmsnorm(h, norm_ple[i]) @ ple_gate[i])
        h = h + (p[i] @ ple_proj[i]) * gate
    return rmsnorm(h, final_norm)
```

```python
import contextlib
import math
import numpy as np
import concourse.bass as bass
import concourse.mybir as mybir
from concourse.bass_utils import run_bass_kernel_spmd

F32 = mybir.dt.float32
BF16 = mybir.dt.bfloat16
AF = mybir.ActivationFunctionType
ALU = mybir.AluOpType

D = 1024
T = 2048
DEPTH = 4
NCORES = 8
EPS = 1e-6


class Clock:
    __slots__ = ("sem", "count", "step", "name")

    def __init__(self, sem, step, name):
        self.sem = sem
        self.count = 0
        self.step = step
        self.name = name


class Buf:
    __slots__ = ("name", "w", "r")

    def __init__(self, name=""):
        self.name = name
        self.w = None
        self.r = {}


class Eng:
    def __init__(self, name, clock):
        self.name = name
        self.clock = clock
        self.ops = []
        self.seen = {}


class Sched:
    def __init__(self, nc, stack):
        self.nc = nc
        self.stack = stack
        self.eng = {}
        for n in ("pe", "act", "dve", "pool", "sp"):
            sem = stack.enter_context(nc.semaphore("clk_" + n))
            self.eng[n] = Eng(n, Clock(sem, 1, n))
        self.dclk = {}

    def dma_clock(self, key):
        if key not in self.dclk:
            sem = self.stack.enter_context(self.nc.semaphore("dq_" + str(key)))
            self.dclk[key] = Clock(sem, 16, "dma_" + str(key))
        return self.dclk[key]

    def _need(self, e, reads, writes):
        need = {}

        def add(cv):
            if cv is None:
                return
            c, v = cv
            if need.get(c, 0) < v:
                need[c] = v

        for b in reads:
            add(b.w)
        for b in writes:
            add(b.w)
            for c, v in b.r.items():
                add((c, v))
        out = []
        for c, v in need.items():
            if c is e.clock:
                if e.name == "pe":
                    continue
                if v < c.count:
                    continue
            if e.seen.get(c, 0) >= v:
                continue
            e.seen[c] = v
            out.append((c, v))
        return out

    def op(self, en, fn, reads=(), writes=(), inc=True):
        e = self.eng[en]
        for c, v in self._need(e, reads, writes):
            e.ops.append(("w", c.sem, v))
        clk = e.clock
        val = clk.count + 1
        e.ops.append(("o", fn, clk.sem if inc else None, 1))
        for b in reads:
            if b.r.get(clk, 0) < val:
                b.r[clk] = val
        for b in writes:
            b.w = (clk, val)
            b.r = {}
        if inc:
            clk.count = val

    def dma(self, qn, key, out, in_, reads=(), writes=(), slow=False, batch=False):
        e = self.eng[qn]
        for c, v in self._need(e, reads, writes):
            e.ops.append(("w", c.sem, v))
        clk = self.dma_clock(key)
        if not batch and clk.count > 0 and e.seen.get(clk, 0) < clk.count:
            e.ops.append(("w", clk.sem, clk.count))
            e.seen[clk] = clk.count
        val = clk.count + 16
        if slow:
            e.ops.append(("o", (lambda E, o=out, i=in_: E.dma_start(out=o, in_=i, allow_slow_non_contiguous=True)),
                          clk.sem, 16))
        else:
            e.ops.append(("o", (lambda E, o=out, i=in_: E.dma_start(out=o, in_=i)), clk.sem, 16))
        for b in reads:
            if b.r.get(clk, 0) < val:
                b.r[clk] = val
        for b in writes:
            b.w = (clk, val)
            b.r = {}
        clk.count = val

    def fence(self):
        clocks = [e.clock for e in self.eng.values()] + list(self.dclk.values())
        for e in self.eng.values():
            for c in clocks:
                if c is e.clock or c.count == 0:
                    continue
                if e.seen.get(c, 0) < c.count:
                    e.ops.append(("w", c.sem, c.count))
                    e.seen[c] = c.count

    def emit(self):
        nc = self.nc
        with nc.Block() as block:
            def mk(en):
                ops = self.eng[en].ops

                def body(E):
                    for o in ops:
                        if o[0] == "w":
                            E.wait_ge(o[1], o[2])
                        else:
                            ins = o[1](E)
                            if o[2] is not None:
                                ins.then_inc(o[2], o[3])
                return body
            block.tensor(mk("pe"))
            block.scalar(mk("act"))
            block.vector(mk("dve"))
            block.gpsimd(mk("pool"))
            block.sync(mk("sp"))


WSPEC = [
    ("norm_mix", [4, D]), ("norm_mlp", [4, D]), ("norm_ple", [4, D]),
    ("s5_w_in", [2, D, D]), ("s5_lam_re", [2, 64, 64]), ("s5_lam_im", [2, 64, 64]),
    ("s5_log_dt", [2, 64]), ("s5_b_re", [2, 64, 64, 16]), ("s5_b_im", [2, 64, 64, 16]),
    ("s5_c_re", [2, 64, 16, 64]), ("s5_c_im", [2, 64, 16, 64]), ("s5_d", [2, D]),
    ("s5_w_glu", [2, D, 2 * D]), ("s5_w_out", [2, D, D]),
    ("fox_w_in", [1, D, 3088]), ("fox_b_f", [1, 16]), ("fox_w_out", [1, D, D]),
    ("gla_w_in", [1, D, 3088]), ("gla_w_g2", [1, 16, 512]), ("gla_b_g", [1, 512]),
    ("gla_norm", [1, D]), ("gla_w_out", [1, D, D]),
    ("mlp_w1", [4, D, 4 * D]), ("mlp_w2", [4, 4 * D, D]),
    ("ple_proj", [4, 256, D]), ("ple_gate", [4, D, D]), ("final_norm", [D]),
]


class Prog:
    def __init__(self, nseq, layers=(0, 1, 2, 3), mixers=True, mlp=True, ple=True, dbg=0):
        self.nseq = nseq
        self.dbg = dbg
        self.layers = layers
        self.mixers = mixers
        self.do_mlp = mlp
        self.do_ple = ple
        self.nc = bass.Bass("TRN2", target_bir_lowering=False)
        nc = self.nc
        self.x = nc.dram_tensor("x", [nseq, T, D], F32, kind="ExternalInput").ap()
        self.p = nc.dram_tensor("p", [DEPTH, nseq, T, 256], F32, kind="ExternalInput").ap()
        self.w = {}
        for name, shape in WSPEC:
            self.w[name] = nc.dram_tensor(name, shape, F32, kind="ExternalInput").ap()
        self.out = nc.dram_tensor("out", [nseq, T, D], F32, kind="ExternalOutput").ap()
        with contextlib.ExitStack() as st:
            self.st = st
            self.S = Sched(nc, st)
            self.alloc()
            self.consts()
            for s in range(nseq):
                self.sequence(s)
            S = self.S
            S.fence()
            S.emit()

    def tile(self, name, shape, dt=F32):
        return self.st.enter_context(self.nc.sbuf_tensor(name, shape, dt))

    def alloc(self):
        nc, st = self.nc, self.st
        self.hT = self.tile("hT", [128, 8, T], F32)
        self.hB = [[Buf("h%d_%d" % (c, tt)) for tt in range(4)] for c in range(8)]
        self.A = self.tile("A", [128, 8, T], BF16)
        self.AB = [[Buf("A%d_%d" % (c, tt)) for tt in range(4)] for c in range(8)]
        self.Alo = self.tile("Alo", [128, 8, 256], BF16)
        self.AloB = [Buf("Alo%d" % c) for c in range(8)]
        self.NSLOT = 3
        self.ring = [self.tile("ring%d" % i, [128, 4096], BF16) for i in range(self.NSLOT)]
        self.ringB = [Buf("ring%d" % i) for i in range(self.NSLOT)]
        self.ring_i = 0
        self.psA = st.enter_context(nc.psum_tensor("psA", [128, 6, 512], F32))
        self.psAB = [Buf("psA%d" % i) for i in range(6)]
        self.NROT = 4
        self.psA_i = 0
        self.psT = st.enter_context(nc.psum_tensor("psT", [128, 2, 1024], BF16))
        self.psTB = [Buf("psT%d" % i) for i in range(2)]
        self.psT_i = 0
        self.sq = [self.tile("sq%d" % i, [128, 512], BF16) for i in range(2)]
        self.sqB = [Buf() for _ in range(2)]
        self.sq_i = 0
        self.f32t = [self.tile("f32t%d" % i, [128, 512], F32) for i in range(4)]
        self.f32tB = [Buf() for _ in range(4)]
        self.f32t_i = 0
        self.B = self.tile("B", [128, 32768], BF16)
        self.aT = [self.bv(i * 2048, [4, 512]) for i in range(2)]
        self.aTB = [Buf() for _ in range(2)]
        self.stage = [self.bv(i * 2048, [D], F32) for i in range(2)]
        self.stageB = [Buf() for _ in range(2)]
        self.pstage = self.bv(8192, [16, 256])
        self.pstageB = Buf()
        self.pT = self.bv(12288, [2, T])
        self.pTB = Buf()
        self.evac_i = 0
        self.Acoef = self.tile("Acoef", [128, 2, 2, 32], F32)
        self.Tt = self.tile("Tt", [128, 2, 2, 32], F32)
        self.U2 = self.tile("U2", [128, 2, 32], F32)
        self.Hs = [self.tile("Hs%d" % i, [128, 2, 32], F32) for i in range(2)]
        self.acB = Buf("Acoef")
        self.s5B = (Buf("Bm"), Buf("Cm"))
        self.diagD = self.tile("diagD", [128, 8, 128], BF16)
        self.rowm = self.tile("rowm", [128, 4], F32)
        self.colm = self.tile("colm", [128, 4, 64], F32)
        print("sbuf bytes remaining", nc.sbuf_bytes_remaining)

    def bv(self, off, shape, dt=BF16, parts=(0, 128)):
        n = int(np.prod(shape))
        mul = 2 if dt == F32 else 1
        ap = self.B[parts[0]:parts[1], off:off + n * mul]
        if dt == F32:
            ap = ap.bitcast(F32)
        if len(shape) == 1:
            return ap
        names = " ".join("d%d" % i for i in range(len(shape)))
        kw = {"d%d" % i: shape[i] for i in range(len(shape) - 1)}
        return ap.rearrange("p (%s) -> p %s" % (names, names), **kw)

    def bank(self):
        i = self.psA_i
        self.psA_i = (i + 1) % self.NROT
        return self.psA[:, i, :], self.psAB[i]

    def dbank(self, k):
        return self.psA[:, 4 + k, :], self.psAB[4 + k]

    def tbank(self):
        i = self.psT_i
        self.psT_i = (i + 1) % 2
        return self.psT[:, i, :], self.psTB[i]

    def tmp32(self):
        i = self.f32t_i
        self.f32t_i = (i + 1) % 3
        return self.f32t[i], self.f32tB[i]

    def evac_eng(self):
        self.evac_i ^= 1
        return "act" if self.evac_i else "dve"

    def load_w(self, src3, shape):
        i = self.ring_i
        self.ring_i = (i + 1) % self.NSLOT
        a, b = shape
        assert a * b <= 4096
        view = self.ring[i][:, 0:a * b].rearrange("p (a b) -> p a b", a=a)
        self.S.dma("pool", "ring%d" % i, view, src3, writes=[self.ringB[i]])
        return view, self.ringB[i]

    def consts(self):
        S, nc = self.S, self.nc
        self.ident = self.tile("ident", [128, 128], F32)
        self.identb = self.tile("identb", [128, 128], BF16)
        self.onesb = self.tile("onesb", [128, 128], BF16)

        self.epsT = self.tile("epsT", [128, 1], F32)
        self.cB = Buf("consts")
        cB = self.cB
        S.op("pool", lambda E: E.memset(self.ident[:], 1.0), [], [cB])
        S.op("pool", lambda E: E.affine_select(out=self.ident[:], in_=self.ident[:], pattern=[[-1, 128]],
                                               compare_op=ALU.is_equal, fill=0.0, base=0,
                                               channel_multiplier=1), [cB], [cB])
        S.op("pool", lambda E: E.memset(self.onesb[:], 1.0), [], [cB])

        S.op("pool", lambda E: E.memset(self.epsT[:], EPS), [], [cB])
        S.op("dve", lambda E: E.tensor_copy(out=self.identb[:], in_=self.ident[:]), [cB], [cB])
        vecs = []
        for i in range(4):
            vecs.append(("mix%d" % i, self.w["norm_mix"][i]))
            vecs.append(("mlp%d" % i, self.w["norm_mlp"][i]))
            vecs.append(("ple%d" % i, self.w["norm_ple"][i]))
        vecs.append(("final", self.w["final_norm"]))
        vecs.append(("gla_norm", self.w["gla_norm"][0]))
        vecs.append(("s5d0", self.w["s5_d"][0]))
        vecs.append(("s5d1", self.w["s5_d"][1]))
        self.gidx = {n: i for i, (n, _) in enumerate(vecs)}
        self.gains = self.tile("gains", [128, len(vecs), 8], F32)
        for i, (n, ap) in enumerate(vecs):
            S.dma("sp", "misc", self.gains[:, i, :], ap.rearrange("(c p) -> p c", p=128), writes=[cB], slow=True)
        S.fence()

    def gain(self, name, c):
        return self.gains[:, self.gidx[name], c:c + 1]

    def load_x(self, s):
        S = self.S
        for tb in range(16):
            stg, sb = self.stage[tb % 2], self.stageB[tb % 2]
            S.dma("sp", "stage%d" % (tb % 2), stg[:], self.x[s, tb * 128:(tb + 1) * 128, :], writes=[sb])
            for half in range(2):
                ps, pb = self.bank()
                for q in range(4):
                    c = half * 4 + q
                    S.op("pe", lambda E, ps=ps, q=q, c=c, stg=stg: E.transpose(
                        out=ps[:, q * 128:(q + 1) * 128], in_=stg[:, c * 128:(c + 1) * 128],
                        identity=self.ident[:]), [sb, self.cB], [pb], inc=(q == 3))
                dst = self.hT[:, half * 4:half * 4 + 4, tb * 128:(tb + 1) * 128]
                src = ps.rearrange("p (q t) -> p q t", q=4)
                wb = [self.hB[half * 4 + q][tb // 4] for q in range(4)]
                if self.evac_eng() == "act":
                    S.op("act", lambda E, dst=dst, src=src: E.activation(out=dst, in_=src, func=AF.Copy), [pb], wb)
                else:
                    S.op("dve", lambda E, dst=dst, src=src: E.tensor_copy(out=dst, in_=src), [pb], wb)

    def rstd_tile(self, tt, src_tile, srcB, nchunks, chunk0=0, scale=1.0 / D):
        S = self.S
        ps, pb = self.bank()
        for k in range(nchunks):
            c = chunk0 + k
            i = self.sq_i
            self.sq_i ^= 1
            sq, sqb = self.sq[i], self.sqB[i]
            S.op("act", lambda E, sq=sq, c=c: E.activation(
                out=sq[:], in_=src_tile[:, c, tt * 512:(tt + 1) * 512], func=AF.Square), [srcB[c][tt]], [sqb])
            S.op("pe", lambda E, ps=ps, sq=sq, k=k: E.matmul(ps, self.onesb[:], sq[:], start=(k == 0),
                                                            stop=(k == nchunks - 1)),
                 [sqb, self.cB], [pb])
        rt, rtb = self.f32t[3], self.f32tB[3]
        S.op("act", lambda E, rt=rt, ps=ps: E.activation(out=rt[:], in_=ps, func=AF.Sqrt, bias=self.epsT[:],
                                                         scale=scale), [pb, self.cB], [rtb])
        S.op("dve", lambda E, rt=rt: E.reciprocal(out=rt[:], in_=rt[:]), [rtb], [rtb])
        return rt, rtb

    def rmsnorm_to_A(self, gname):
        S = self.S
        k = 0
        for tt in range(4):
            rt, rtb = self.rstd_tile(tt, self.hT, self.hB, 8)
            for c in range(8):
                en = "dve"
                if tt == 0:
                    t32, tb = self.tmp32()
                    S.op("dve", lambda E, c=c, rt=rt, t32=t32: E.scalar_tensor_tensor(
                        out=t32[:], in0=self.hT[:, c, 0:512], scalar=self.gain(gname, c), in1=rt[:], op0=ALU.mult,
                        op1=ALU.mult), [self.hB[c][0], rtb, self.cB], [tb])
                    S.op("act", lambda E, c=c, t32=t32: E.activation(out=self.A[:, c, 0:512], in_=t32[:], func=AF.Copy),
                         [tb], [self.AB[c][0]])
                    S.op("dve", lambda E, c=c, t32=t32: E.tensor_tensor(out=self.Alo[:, c, :], in0=t32[:, 0:256],
                                                                      in1=self.A[:, c, 0:256], op=ALU.subtract),
                         [tb, self.AB[c][0]], [self.AloB[c]])
                    continue
                S.op(en, lambda E, c=c, tt=tt, rt=rt: E.scalar_tensor_tensor(
                    out=self.A[:, c, tt * 512:(tt + 1) * 512], in0=self.hT[:, c, tt * 512:(tt + 1) * 512],
                    scalar=self.gain(gname, c), in1=rt[:], op0=ALU.mult, op1=ALU.mult),
                    [self.hB[c][tt], rtb, self.cB], [self.AB[c][tt]])

    def wview(self, W2d, c0, ncols):
        return W2d.rearrange("(kc p) n -> p kc n", p=128)[:, :, c0:c0 + ncols]

    def add_to_h(self, ps, pb, c, tt):
        S = self.S
        S.op("dve", lambda E, ps=ps, c=c, tt=tt: E.tensor_tensor(
            out=self.hT[:, c, tt * 512:(tt + 1) * 512], in0=self.hT[:, c, tt * 512:(tt + 1) * 512], in1=ps,
            op=ALU.add), [pb, self.hB[c][tt]], [self.hB[c][tt]])

    def mlp(self, li):
        S = self.S
        W1 = self.w["mlp_w1"][li]
        W2 = self.w["mlp_w2"][li]
        it = 0
        for sl in range(8):
            w1, w1b = self.load_w(self.wview(W1, sl * 512, 512), (8, 512))
            w2, w2b = self.load_w(W2[sl * 512:(sl + 1) * 512, :].rearrange("(kc p) n -> p kc n", p=128), (4, 1024))
            for tt in range(4):
                aT, aTb = self.aT[it % 2], self.aTB[it % 2]
                it += 1
                for m in range(4):
                    ps, pb = self.bank()
                    for kc in range(8):
                        last = (kc == 7) and tt != 0
                        S.op("pe", lambda E, ps=ps, w1=w1, kc=kc, m=m, tt=tt, last=last: E.matmul(
                            ps, w1[:, kc, m * 128:(m + 1) * 128], self.A[:, kc, tt * 512:(tt + 1) * 512],
                            start=(kc == 0), stop=last), [w1b, self.AB[kc][tt]], [pb], inc=last)
                    if tt == 0:
                        for kc in range(8):
                            S.op("pe", lambda E, ps=ps, w1=w1, kc=kc, m=m: E.matmul(
                                ps[:, 0:256], w1[:, kc, m * 128:(m + 1) * 128], self.Alo[:, kc, :], start=False, stop=(kc == 7)),
                                [w1b, self.AloB[kc]], [pb], inc=(kc == 7))
                    r, rb = self.tmp32()
                    S.op("act", lambda E, r=r, ps=ps: E.activation(out=r[:], in_=ps, func=AF.Relu), [pb], [rb])
                    S.op("pool", lambda E, r=r, aT=aT, m=m: E.tensor_tensor(out=aT[:, m, :], in0=r[:], in1=r[:],
                                                                          op=ALU.mult), [rb], [aTb])
                for oc in range(8):
                    ps, pb = self.bank()
                    for m in range(4):
                        S.op("pe", lambda E, ps=ps, w2=w2, m=m, oc=oc, aT=aT: E.matmul(
                            ps, w2[:, m, oc * 128:(oc + 1) * 128], aT[:, m, :], start=(m == 0), stop=(m == 3)),
                            [w2b, aTb], [pb], inc=(m == 3))
                    self.add_to_h(ps, pb, oc, tt)

    def load_pT(self, li, s):
        S = self.S
        src = self.p[li, s].rearrange("(blk p) n -> p blk n", p=128)
        S.dma("pool", "pstage", self.pstage[:], src, writes=[self.pstageB])
        for fc in range(2):
            for g in range(2):
                ps, pb = self.tbank()
                for q in range(8):
                    blk = g * 8 + q
                    S.op("pe", lambda E, ps=ps, q=q, blk=blk, fc=fc: E.transpose(
                        out=ps[:, q * 128:(q + 1) * 128], in_=self.pstage[:, blk, fc * 128:(fc + 1) * 128],
                        identity=self.identb[:]), [self.pstageB, self.cB], [pb], inc=(q == 7))
                dst = self.pT[:, fc, g * 1024:(g + 1) * 1024]
                if self.evac_eng() == "act":
                    S.op("act", lambda E, dst=dst, ps=ps: E.activation(out=dst, in_=ps, func=AF.Copy), [pb], [self.pTB])
                else:
                    S.op("dve", lambda E, dst=dst, ps=ps: E.tensor_copy(out=dst, in_=ps), [pb], [self.pTB])

    def ple(self, li, s):
        S = self.S
        self.load_pT(li, s)
        Wg = self.w["ple_gate"][li]
        Wp = self.w["ple_proj"][li]
        for sl in range(2):
            wg, wgb = self.load_w(self.wview(Wg, sl * 512, 512), (8, 512))
            wp, wpb = self.load_w(self.wview(Wp, sl * 512, 512), (2, 512))
            for tt in range(4):
                for m in range(4):
                    oc = sl * 4 + m
                    ps, pb = self.bank()
                    for kc in range(8):
                        last = (kc == 7) and tt != 0
                        S.op("pe", lambda E, ps=ps, wg=wg, kc=kc, m=m, tt=tt, last=last: E.matmul(
                            ps, wg[:, kc, m * 128:(m + 1) * 128], self.A[:, kc, tt * 512:(tt + 1) * 512],
                            start=(kc == 0), stop=last), [wgb, self.AB[kc][tt]], [pb], inc=last)
                    if tt == 0:
                        for kc in range(8):
                            S.op("pe", lambda E, ps=ps, wg=wg, kc=kc, m=m: E.matmul(
                                ps[:, 0:256], wg[:, kc, m * 128:(m + 1) * 128], self.Alo[:, kc, :], start=False, stop=(kc == 7)),
                                [wgb, self.AloB[kc]], [pb], inc=(kc == 7))
                    g, gb = self.tmp32()
                    S.op("act", lambda E, g=g, ps=ps: E.activation(out=g[:], in_=ps, func=AF.Sigmoid), [pb], [gb])
                    ps2, pb2 = self.bank()
                    for kc in range(2):
                        S.op("pe", lambda E, ps2=ps2, wp=wp, kc=kc, m=m, tt=tt: E.matmul(
                            ps2, wp[:, kc, m * 128:(m + 1) * 128], self.pT[:, kc, tt * 512:(tt + 1) * 512],
                            start=(kc == 0), stop=(kc == 1)), [wpb, self.pTB], [pb2], inc=(kc == 1))
                    S.op("dve", lambda E, g=g, ps2=ps2: E.tensor_tensor(out=g[:], in0=g[:], in1=ps2, op=ALU.mult),
                         [gb, pb2], [gb])
                    S.op("pool", lambda E, g=g, oc=oc, tt=tt: E.tensor_tensor(
                        out=self.hT[:, oc, tt * 512:(tt + 1) * 512], in0=self.hT[:, oc, tt * 512:(tt + 1) * 512],
                        in1=g[:], op=ALU.add), [gb, self.hB[oc][tt]], [self.hB[oc][tt]])

    def final(self, s):
        S = self.S
        fA = self.A[:].rearrange("p c n -> p (c n)").bitcast(F32).rearrange("p (c n) -> p c n", c=8)
        for tt in range(4):
            rt, rtb = self.rstd_tile(tt, self.hT, self.hB, 8)
            half = tt % 2
            for c in range(8):
                S.op("dve", lambda E, c=c, tt=tt, rt=rt, half=half: E.scalar_tensor_tensor(
                    out=fA[:, c, half * 512:(half + 1) * 512], in0=self.hT[:, c, tt * 512:(tt + 1) * 512],
                    scalar=self.gain("final", c), in1=rt[:], op0=ALU.mult, op1=ALU.mult),
                    [self.hB[c][tt], rtb, self.cB], [self.AB[c][half]])
            for tbl in range(4):
                tb = tt * 4 + tbl
                stg, sb = self.stage[tb % 2], self.stageB[tb % 2]
                for hf in range(2):
                    ps, pb = self.bank()
                    for q in range(4):
                        c = hf * 4 + q
                        S.op("pe", lambda E, ps=ps, q=q, c=c, tbl=tbl, half=half: E.transpose(
                            out=ps[:, q * 128:(q + 1) * 128],
                            in_=fA[:, c, half * 512 + tbl * 128: half * 512 + (tbl + 1) * 128],
                            identity=self.ident[:]), [self.AB[c][half], self.cB], [pb], inc=(q == 3))
                    dst = stg[:, hf * 512:(hf + 1) * 512]
                    if self.evac_eng() == "act":
                        S.op("act", lambda E, dst=dst, ps=ps: E.activation(out=dst, in_=ps, func=AF.Copy), [pb], [sb])
                    else:
                        S.op("dve", lambda E, dst=dst, ps=ps: E.tensor_copy(out=dst, in_=ps), [pb], [sb])
                S.dma("sp", "ostage%d" % (tb % 2), self.out[s, tb * 128:(tb + 1) * 128, :], stg[:], reads=[sb])

    def sequence(self, s):
        S = self.S
        if self.dbg == 1:
            S.dma("sp", "dbg", self.out[s, 0:128, 0:8 * len(self.gidx)], self.gains[:].rearrange("p v c -> p (v c)"), reads=[self.cB])
            return
        self.load_x(s)
        if self.dbg == 3:
            S.fence()
            rt, rtb = self.rstd_tile(0, self.hT, self.hB, 8)
            S.dma("sp", "dbg", self.out[s, 0:128, 0:512], rt[:], reads=[rtb])
            return
        if self.dbg == 2:
            S.fence()
            for c in range(8):
                S.dma("sp", "dbg", self.out[s, c * 128:(c + 1) * 128, :], self.hT[:, c, 0:1024], reads=[self.hB[c][0], self.hB[c][1]])
            return
        for li in self.layers:
            if self.mixers:
                if li % 3 == 0:
                    self.s5_prologue(li // 3)
                self.rmsnorm_to_A("mix%d" % li)
                S.fence()
                m = li % 3
                if m == 0:
                    self.s5(li // 3, s)
                elif m == 1:
                    self.fox(li // 3, s)
                else:
                    self.gla(li // 3, s)
                S.fence()
            if self.do_mlp:
                self.rmsnorm_to_A("mlp%d" % li)
                self.mlp(li)
            if self.do_ple:
                self.rmsnorm_to_A("ple%d" % li)
                self.ple(li, s)
        S.fence()
        if self.dbg == 6:
            for c in range(8):
                S.dma("sp", "dbg", self.out[s, c * 128:(c + 1) * 128, :], self.hT[:, c, 0:1024], reads=[self.hB[c][0], self.hB[c][1]])
            S.fence()
            return
        if self.dbg == 3:
            rt, rtb = self.rstd_tile(0, self.hT, self.hB, 8)
            S.dma("sp", "dbg", self.out[s, 0:128, 0:512], rt[:], reads=[rtb])
            return
        self.final(s)
        S.fence()

    def evac(self, dst, src, reads, writes):
        S = self.S
        if self.evac_eng() == "act":
            S.op("act", lambda E: E.activation(out=dst, in_=src, func=AF.Copy), reads, writes)
        else:
            S.op("dve", lambda E: E.tensor_copy(out=dst, in_=src), reads, writes)

    def load_w2(self, srcs, shape):
        i = self.ring_i
        self.ring_i = (i + 1) % self.NSLOT
        a, b = shape
        assert a * b <= 4096
        view = self.ring[i][:, 0:a * b].rearrange("p (a b) -> p a b", a=a)
        o = 0
        for src, nb in srcs:
            self.S.dma("pool", "ring%d" % i, view[:, :, o:o + nb], src, writes=[self.ringB[i]], batch=(o > 0))
            o += nb
        return view, self.ringB[i]

    def proj_fm(self, w, wb, m, tt, kcs=8, src=None, srcB=None):
        S = self.S
        src = self.A if src is None else src
        ps, pb = self.bank()
        lo = (srcB is None and tt == 0)
        for kc in range(kcs):
            rb = self.AB[kc][tt] if srcB is None else srcB
            last = (kc == kcs - 1) and not lo
            S.op("pe", lambda E, ps=ps, kc=kc, last=last: E.matmul(
                ps, w[:, kc, m * 128:(m + 1) * 128], src[:, kc, tt * 512:(tt + 1) * 512],
                start=(kc == 0), stop=last), [wb, rb], [pb], inc=last)
        if lo:
            for kc in range(kcs):
                S.op("pe", lambda E, ps=ps, kc=kc: E.matmul(
                    ps[:, 0:256], w[:, kc, m * 128:(m + 1) * 128], self.Alo[:, kc, :], start=False, stop=(kc == kcs - 1)),
                    [wb, self.AloB[kc]], [pb], inc=(kc == kcs - 1))
        return ps, pb

    def proj_tm(self, w, wb, blk, ncols):
        S = self.S
        ps, pb = self.bank()
        lo = blk < 2
        for kc in range(8):
            last = (kc == 7) and not lo
            S.op("pe", lambda E, ps=ps, kc=kc, last=last: E.matmul(
                ps[:, 0:ncols], self.A[:, kc, blk * 128:(blk + 1) * 128], w[:, kc, 0:ncols],
                start=(kc == 0), stop=last), [wb, self.AB[kc][blk // 4]], [pb], inc=last)
        if lo:
            for kc in range(8):
                S.op("pe", lambda E, ps=ps, kc=kc: E.matmul(
                    ps[:, 0:ncols], self.Alo[:, kc, blk * 128:(blk + 1) * 128], w[:, kc, 0:ncols],
                    start=False, stop=(kc == 7)), [wb, self.AloB[kc]], [pb], inc=(kc == 7))
        return ps[:, 0:ncols], pb

    def out_proj(self, Wrows, kcs, srcT, srcB):
        S = self.S
        for half in range(2):
            w, wb = self.load_w(Wrows.rearrange("(kc p) n -> p kc n", p=128)[:, :, half * 512:(half + 1) * 512],
                                (kcs, 512))
            for tt in range(4):
                for m in range(4):
                    oc = half * 4 + m
                    ps, pb = self.bank()
                    for kc in range(kcs):
                        S.op("pe", lambda E, ps=ps, kc=kc, m=m, tt=tt, w=w: E.matmul(
                            ps, w[:, kc, m * 128:(m + 1) * 128], srcT[:, kc, tt * 512:(tt + 1) * 512],
                            start=(kc == 0), stop=(kc == kcs - 1)), [wb, srcB], [pb], inc=(kc == kcs - 1))
                    self.add_to_h(ps, pb, oc, tt)

    def fox(self, j, s):
        S = self.S
        Win = self.w["fox_w_in"][j]
        Wv3 = Win.rearrange("(kc p) n -> p kc n", p=128)
        qT = self.bv(0, [2, T]); kT = self.bv(4096, [2, T]); v = self.bv(8192, [16, 256]); oT = self.bv(12288, [2, T])
        Fc = self.bv(16384, [T], F32, (0, 16)); lf = self.bv(20480, [T], F32, (0, 16))
        nFt = self.bv(24576, [256], F32)
        masks = self.bv(25088, [4, 512])
        Fsplit = [self.bv(20480 + i * 1536, [3, 512], BF16, (0, 16)) for i in range(2)]
        onesf = self.bv(27136, [128], F32, (0, 16))
        Fm = [self.bv(27392 + i * 1024, [512], F32, (0, 16)) for i in range(2)]
        ones16 = self.bv(29440, [512], F32, (0, 16))
        nbf = self.bv(30464, [1], F32, (0, 16))
        fqT = self.bv(30720, [512], F32)
        fqTB = Buf()
        qB, kB, vB, oB, FB, lfB, nFtB, mB, cB2 = Buf(), Buf(), Buf(), Buf(), Buf(), Buf(), Buf(), Buf(), Buf()
        FmB = [Buf(), Buf()]
        S.op("pool", lambda E: E.memset(onesf, 1.0), [], [cB2])
        S.op("pool", lambda E: E.memset(ones16, 1.0), [], [cB2])
        S.dma("sp", "misc", nbf, self.w["fox_b_f"][j].rearrange("(h o) -> h o", o=1), writes=[cB2])
        S.op("dve", lambda E: E.tensor_scalar(out=nbf, in0=nbf, scalar1=-1.0, scalar2=None, op0=ALU.mult), [cB2], [cB2])
        for r in range(4):
            t32, tb = self.tmp32()
            S.op("pool", lambda E, t32=t32: E.memset(t32[:], 1.0), [], [tb])
            S.op("pool", lambda E, t32=t32, r=r: E.affine_select(
                out=t32[:], in_=t32[:], pattern=[[1, 512]], compare_op=ALU.is_ge, fill=0.0, base=-128 * r,
                channel_multiplier=-1), [tb], [tb])
            S.op("dve", lambda E, t32=t32, r=r: E.tensor_copy(out=masks[:, r, :], in_=t32[:]), [tb], [mB])
        wf, wfb = self.load_w(Wv3[:, :, 3072:3088], (8, 16))
        for tt in range(4):
            ps, pb = self.bank()
            for kc in range(8):
                S.op("pe", lambda E, ps=ps, kc=kc, tt=tt: E.matmul(
                    ps[0:16, :], wf[:, kc, 0:16], self.A[:, kc, tt * 512:(tt + 1) * 512],
                    start=(kc == 0), stop=(kc == 7)), [wfb, self.AB[kc][tt]], [pb])
            sl = slice(tt * 512, (tt + 1) * 512)
            S.op("act", lambda E, ps=ps, sl=sl: E.activation(out=lf[:, sl], in_=ps[0:16, :], func=AF.Exp, bias=nbf,
                                                             scale=-1.0), [pb, cB2], [lfB])
            S.op("act", lambda E, sl=sl: E.activation(out=lf[:, sl], in_=lf[:, sl], func=AF.Ln, bias=1.0, scale=1.0),
                 [lfB], [lfB])
            init = 0.0 if tt == 0 else Fc[:, tt * 512 - 1:tt * 512]
            S.op("dve", lambda E, sl=sl, init=init: E.tensor_tensor_scan(
                out=Fc[:, sl], data0=ones16, data1=lf[:, sl], initial=init, op0=ALU.mult, op1=ALU.subtract),
                [lfB, cB2, FB], [FB])
        ps, pb = self.bank()
        for blk in range(16):
            S.op("pe", lambda E, ps=ps, blk=blk: E.transpose(
                out=ps[:, blk * 16:(blk + 1) * 16], in_=Fc[:, blk * 128:(blk + 1) * 128],
                identity=self.ident[0:16, 0:16]), [FB, self.cB], [pb])
        S.op("dve", lambda E, ps=ps: E.tensor_scalar(out=nFt, in0=ps[:, 0:256], scalar1=-1.0, scalar2=None,
                                                     op0=ALU.mult), [pb], [nFtB])
        for qp in range(4):
            S.fence()
            wqk, wqkb = self.load_w2([(Wv3[:, :, 256 * qp:256 * qp + 256], 256),
                                      (Wv3[:, :, 1024 + 256 * qp:1024 + 256 * qp + 256], 256)], (8, 512))
            wv, wvb = self.load_w(Wv3[:, :, 2048 + 256 * qp:2048 + 256 * qp + 256], (8, 256))
            for tt in range(4):
                for m in range(4):
                    ps, pb = self.proj_fm(wqk, wqkb, m, tt)
                    dst = (qT if m < 2 else kT)[:, m % 2, tt * 512:(tt + 1) * 512]
                    self.evac(dst, ps, [pb], [qB if m < 2 else kB])
            for blk in range(16):
                ps, pb = self.proj_tm(wv, wvb, blk, 256)
                self.evac(v[:, blk, :], ps, [pb], [vB])
            for hl in range(4):
                h = 4 * qp + hl
                ch, base = hl // 2, 64 * (hl % 2)
                for tt in range(4):
                    fm, fmb = Fm[tt % 2], FmB[tt % 2]
                    S.op("dve", lambda E, fm=fm, tt=tt, h=h: E.tensor_scalar(
                        out=fm, in0=Fc[:, tt * 512:(tt + 1) * 512], scalar1=self.ident[0:16, h:h + 1], scalar2=None,
                        op0=ALU.mult), [FB, self.cB], [fmb])
                    fs = Fsplit[tt % 2]
                    S.op("dve", lambda E, fs=fs, fm=fm: E.tensor_copy(out=fs[:, 0, :], in_=fm), [fmb], [fmb])
                    S.op("dve", lambda E, fs=fs, fm=fm: E.tensor_tensor(out=fm, in0=fm, in1=fs[:, 0, :], op=ALU.subtract),
                         [fmb], [fmb])
                    S.op("dve", lambda E, fs=fs, fm=fm: E.tensor_copy(out=fs[:, 1, :], in_=fm), [fmb], [fmb])
                    S.op("dve", lambda E, fs=fs, fm=fm: E.tensor_tensor(out=fm, in0=fm, in1=fs[:, 1, :], op=ALU.subtract),
                         [fmb], [fmb])
                    S.op("dve", lambda E, fs=fs, fm=fm: E.tensor_copy(out=fs[:, 2, :], in_=fm), [fmb], [fmb])
                    ps, pb = self.bank()
                    for k3 in range(3):
                        S.op("pe", lambda E, ps=ps, fs=fs, k3=k3: E.matmul(ps, self.onesb[0:16, :], fs[:, k3, :],
                                                                          start=(k3 == 0), stop=(k3 == 2)),
                             [fmb, self.cB], [pb])
                    fq, fqb = fqT, fqTB
                    S.op("act", lambda E, fq=fq, ps=ps: E.activation(out=fq[:], in_=ps, func=AF.Copy), [pb], [fqb])
                    Ops, Ob = self.dbank(0)
                    Dps, Db = self.dbank(1)
                    nkb = 4 * tt + 4
                    for kb in range(nkb):
                        ps, pb = self.bank()
                        S.op("pe", lambda E, ps=ps, kb=kb, tt=tt, ch=ch, base=base: E.matmul(
                            ps, kT[base:base + 64, ch, kb * 128:(kb + 1) * 128],
                            qT[base:base + 64, ch, tt * 512:(tt + 1) * 512], start=True, stop=True), [kB, qB], [pb])
                        sp, spb = self.tmp32()
                        S.op("dve", lambda E, sp=sp, ps=ps, fq=fq: E.scalar_tensor_tensor(
                            out=sp[:], in0=ps, scalar=0.125, in1=fq[:], op0=ALU.mult, op1=ALU.add), [pb, fqb], [spb])
                        i = self.sq_i
                        self.sq_i ^= 1
                        pt, ptb = self.sq[i], self.sqB[i]
                        r = kb - 4 * tt
                        if r >= 0:
                            S.op("dve", lambda E, sp=sp, kb=kb, h=h: E.tensor_scalar(
                                out=sp[:], in0=sp[:], scalar1=nFt[:, kb * 16 + h:kb * 16 + h + 1], scalar2=30.0,
                                op0=ALU.add, op1=ALU.min), [spb, nFtB], [spb])
                            S.op("act", lambda E, pt=pt, sp=sp: E.activation(out=pt[:], in_=sp[:], func=AF.Exp),
                                 [spb], [ptb])
                        else:
                            S.op("act", lambda E, pt=pt, sp=sp, kb=kb, h=h: E.activation(
                                out=pt[:], in_=sp[:], func=AF.Exp, bias=nFt[:, kb * 16 + h:kb * 16 + h + 1],
                                scale=1.0), [spb, nFtB], [ptb])
                        if r >= 0:
                            S.op("pool", lambda E, pt=pt, r=r: E.tensor_tensor(out=pt[:], in0=pt[:], in1=masks[:, r, :],
                                                                             op=ALU.mult), [ptb, mB], [ptb])
                        S.op("pe", lambda E, pt=pt, kb=kb, hl=hl, base=base, Ops=Ops, nkb=nkb: E.matmul(
                            Ops[base:base + 64, :], v[:, kb, hl * 64:(hl + 1) * 64], pt[:], start=(kb == 0),
                            stop=(kb == nkb - 1)), [vB, ptb], [Ob])
                        S.op("pe", lambda E, pt=pt, kb=kb, base=base, Dps=Dps, nkb=nkb: E.matmul(
                            Dps[base:base + 64, :], self.onesb[:, 0:64], pt[:], start=(kb == 0),
                            stop=(kb == nkb - 1)), [self.cB, ptb], [Db])
                    rd, rdb = self.tmp32()
                    S.op("dve", lambda E, rd=rd, Dps=Dps, base=base: E.reciprocal(
                        out=rd[base:base + 64, :], in_=Dps[base:base + 64, :]), [Db], [rdb])
                    S.op("dve", lambda E, rd=rd, Ops=Ops, base=base, ch=ch, tt=tt: E.tensor_tensor(
                        out=oT[base:base + 64, ch, tt * 512:(tt + 1) * 512], in0=Ops[base:base + 64, :],
                        in1=rd[base:base + 64, :], op=ALU.mult), [Ob, rdb], [oB])
            self.out_proj(self.w["fox_w_out"][j][256 * qp:256 * qp + 256, :], 2, oT, oB)

    def gla(self, j, s):
        S = self.S
        Win = self.w["gla_w_in"][j]
        W3 = Win.rearrange("(kc p) n -> p kc n", p=128)
        qd = self.bv(0, [T]); kupdT = self.bv(2048, [T])
        kinvb = self.bv(4096, [T])
        bcum = self.bv(8192, [T], F32); kinv = bcum
        Eb = self.bv(12288, [T], F32)
        v = self.bv(16384, [16, 256]); ktok = self.bv(20480, [16, 128]); oT = self.bv(22528, [2, T])
        glrT = self.bv(26624, [T], BF16, (0, 16))
        state = self.bv(28672, [256], F32); statebf = self.bv(29184, [256])
        ebl = self.bv(29440, [16], F32); rmask = self.bv(29472, [512], F32)
        cmask = self.bv(30496, [128]); attT = [self.bv(30624 + i * 128, [128]) for i in range(2)]
        wg2 = self.bv(30880, [512], BF16, (0, 16)); nbg = self.bv(31392, [4], F32)
        (qB, kiB, kuB, bcB, eB, vB, ktB, oB, glB, stB, sbB, eblB, cB2) = [Buf() for _ in range(13)]
        attB = [Buf(), Buf()]
        kibB = Buf()
        S.dma("pool", "misc2", wg2, self.w["gla_w_g2"][j], writes=[cB2])
        S.dma("sp", "misc", nbg, self.w["gla_b_g"][j].rearrange("(c p) -> p c", p=128), writes=[cB2], slow=True)
        S.op("dve", lambda E: E.tensor_scalar(out=nbg, in0=nbg, scalar1=-1.0, scalar2=None, op0=ALU.mult), [cB2], [cB2])
        S.op("pool", lambda E: E.memset(rmask, 1.0), [], [cB2])
        for i in range(4):
            S.op("pool", lambda E, i=i: E.memset(rmask[:, 128 * i:128 * i + 1], 0.0), [cB2], [cB2])
        t32, tb = self.tmp32()
        S.op("pool", lambda E: E.memset(t32[:, 0:128], 1.0), [], [tb])
        S.op("pool", lambda E: E.affine_select(out=t32[:, 0:128], in_=t32[:, 0:128], pattern=[[1, 128]],
                                               compare_op=ALU.is_ge, fill=0.0, base=0, channel_multiplier=-1), [tb], [tb])
        S.op("dve", lambda E: E.tensor_copy(out=cmask, in_=t32[:, 0:128]), [tb], [cB2])
        wl, wlb = self.load_w(W3[:, :, 3072:3088], (8, 16))
        for tt in range(4):
            ps, pb = self.bank()
            for kc in range(8):
                S.op("pe", lambda E, ps=ps, kc=kc, tt=tt: E.matmul(
                    ps[0:16, :], wl[:, kc, 0:16], self.A[:, kc, tt * 512:(tt + 1) * 512],
                    start=(kc == 0), stop=(kc == 7)), [wlb, self.AB[kc][tt]], [pb])
            self.evac(glrT[:, tt * 512:(tt + 1) * 512], ps[0:16, :], [pb], [glB])
        for h in range(4):
            S.fence()
            for tt in range(4):
                sl = slice(tt * 512, (tt + 1) * 512)
                ps, pb = self.bank()
                S.op("pe", lambda E, ps=ps, sl=sl, h=h: E.matmul(ps, wg2[:, h * 128:(h + 1) * 128], glrT[:, sl],
                                                                 start=True, stop=True), [cB2, glB], [pb])
                t1, t1b = self.tmp32()
                S.op("act", lambda E, t1=t1, ps=ps, h=h: E.activation(out=t1[:], in_=ps, func=AF.Exp,
                                                                      bias=nbg[:, h:h + 1], scale=-1.0), [pb, cB2], [t1b])
                S.op("act", lambda E, t1=t1: E.activation(out=t1[:], in_=t1[:], func=AF.Ln, bias=1.0, scale=1.0),
                     [t1b], [t1b])
                S.op("dve", lambda E, t1=t1: E.tensor_scalar(out=t1[:], in0=t1[:], scalar1=-1.0 / 16.0, scalar2=None,
                                                             op0=ALU.mult), [t1b], [t1b])
                S.op("dve", lambda E, t1=t1, sl=sl: E.tensor_tensor_scan(
                    out=bcum[:, sl], data0=rmask, data1=t1[:], initial=0.0, op0=ALU.mult, op1=ALU.add),
                    [t1b, cB2], [bcB])
                S.op("act", lambda E, sl=sl: E.activation(out=Eb[:, sl], in_=bcum[:, sl], func=AF.Exp), [bcB], [eB])
            S.op("act", lambda E: E.activation(out=ebl, in_=bcum.rearrange("p (c t) -> p c t", t=128)[:, :, 127],
                                               func=AF.Exp), [bcB], [eblB])
            wqk, wqkb = self.load_w2([(W3[:, :, 128 * h:128 * h + 128], 128),
                                      (W3[:, :, 512 + 128 * h:512 + 128 * h + 128], 128)], (8, 256))
            wv, wvb = self.load_w(W3[:, :, 1024 + 256 * h:1024 + 256 * h + 256], (8, 256))
            for tt in range(4):
                sl = slice(tt * 512, (tt + 1) * 512)
                ps, pb = self.proj_fm(wqk, wqkb, 0, tt)
                S.op("dve", lambda E, ps=ps, sl=sl: E.scalar_tensor_tensor(
                    out=qd[:, sl], in0=ps, scalar=128.0 ** -0.5, in1=Eb[:, sl], op0=ALU.mult, op1=ALU.mult),
                    [pb, eB], [qB])
            for tt in range(4):
                sl = slice(tt * 512, (tt + 1) * 512)
                S.op("act", lambda E, sl=sl: E.activation(out=Eb[:, sl], in_=bcum[:, sl], func=AF.Exp, scale=-1.0),
                     [bcB, eB], [eB])
                ps, pb = self.proj_fm(wqk, wqkb, 1, tt)
                S.op("dve", lambda E, ps=ps, sl=sl: E.tensor_tensor(out=kinv[:, sl], in0=ps, in1=Eb[:, sl], op=ALU.mult),
                     [pb, eB, bcB], [kiB, bcB])
                S.op("act", lambda E, sl=sl: E.activation(out=kinvb[:, sl], in_=kinv[:, sl], func=AF.Copy), [kiB], [kibB])
                S.op("dve", lambda E, sl=sl, tt=tt: E.tensor_tensor(
                    out=kupdT[:, sl].rearrange("p (c t) -> p c t", t=128),
                    in0=kinv[:, sl].rearrange("p (c t) -> p c t", t=128),
                    in1=ebl[:, 4 * tt:4 * tt + 4].unsqueeze(2).to_broadcast([128, 4, 128]), op=ALU.mult),
                    [kiB, eblB], [kuB])
            for g in range(2):
                ps, pb = self.tbank()
                for q in range(8):
                    blk = g * 8 + q
                    S.op("pe", lambda E, ps=ps, q=q, blk=blk: E.transpose(
                        out=ps[:, q * 128:(q + 1) * 128], in_=kupdT[:, blk * 128:(blk + 1) * 128],
                        identity=self.identb[:]), [kuB, self.cB], [pb])
                self.evac(ktok[:, g * 8:(g + 1) * 8, :], ps.rearrange("p (q t) -> p q t", q=8), [pb], [ktB])
            for blk in range(16):
                ps, pb = self.proj_tm(wv, wvb, blk, 256)
                self.evac(v[:, blk, :], ps, [pb], [vB])
            S.op("pool", lambda E: E.memset(state, 0.0), [stB], [stB])
            S.op("pool", lambda E: E.memset(statebf, 0.0), [sbB], [sbB])
            for c in range(16):
                cs = slice(c * 128, (c + 1) * 128)
                ps, pb = self.bank()
                S.op("pe", lambda E, ps=ps, cs=cs: E.matmul(ps[:, 0:128], kinvb[:, cs], qd[:, cs], start=True,
                                                            stop=True), [kibB, qB], [pb])
                at, atb = attT[c % 2], attB[c % 2]
                S.op("dve", lambda E, ps=ps, at=at: E.tensor_tensor(out=at, in0=ps[:, 0:128], in1=cmask, op=ALU.mult),
                     [pb, cB2], [atb])
                po, pob = self.bank()
                for m in range(2):
                    S.op("pe", lambda E, po=po, m=m, c=c, at=at: E.matmul(
                        po[:, m * 128:(m + 1) * 128], v[:, c, m * 128:(m + 1) * 128], at, start=True, stop=False),
                        [vB, atb], [pob])
                    S.op("pe", lambda E, po=po, m=m, cs=cs: E.matmul(
                        po[:, m * 128:(m + 1) * 128], statebf[:, m * 128:(m + 1) * 128], qd[:, cs], start=False,
                        stop=True), [sbB, qB], [pob])
                self.evac(oT[:, :, cs], po[:, 0:256].rearrange("p (m t) -> p m t", m=2), [pob], [oB])
                pst, pstb = self.bank()
                S.op("pe", lambda E, pst=pst, c=c: E.matmul(pst[:, 0:256], ktok[:, c, :], v[:, c, :], start=True,
                                                           stop=True), [ktB, vB], [pstb])
                S.op("dve", lambda E, pst=pst, c=c: E.scalar_tensor_tensor(
                    out=state, in0=state, scalar=ebl[:, c:c + 1], in1=pst[:, 0:256], op0=ALU.mult, op1=ALU.add),
                    [pstb, stB, eblB], [stB])
                S.op("act", lambda E: E.activation(out=statebf, in_=state, func=AF.Copy), [stB], [sbB])
            wr, wrb = self.load_w(W3[:, :, 2048 + 256 * h:2048 + 256 * h + 256], (8, 256))
            oBl = [[oB] * 4, [oB] * 4]
            for tt in range(4):
                sl = slice(tt * 512, (tt + 1) * 512)
                rt, rtb = self.rstd_tile(tt, oT, oBl, 2, scale=1.0 / 256)
                for m in range(2):
                    ps, pb = self.proj_fm(wr, wrb, m, tt)
                    sr, srb = self.tmp32()
                    S.op("act", lambda E, sr=sr, ps=ps: E.activation(out=sr[:], in_=ps, func=AF.Silu), [pb], [srb])
                    t1, t1b = self.tmp32()
                    S.op("dve", lambda E, t1=t1, m=m, sl=sl, rt=rt, h=h: E.scalar_tensor_tensor(
                        out=t1[:], in0=oT[:, m, sl], scalar=self.gain("gla_norm", 2 * h + m), in1=rt[:],
                        op0=ALU.mult, op1=ALU.mult), [oB, rtb, self.cB], [t1b])
                    S.op("dve", lambda E, t1=t1, sr=sr, m=m, sl=sl: E.tensor_tensor(
                        out=oT[:, m, sl], in0=t1[:], in1=sr[:], op=ALU.mult), [t1b, srb, oB], [oB])
            self.out_proj(self.w["gla_w_out"][j][256 * h:256 * h + 256, :], 2, oT, oB)

    def av(self, off, shape, dt=F32, parts=(0, 128)):
        n = int(np.prod(shape))
        mul = 2 if dt != BF16 else 1
        ap = self.A[parts[0]:parts[1]].rearrange("p c n -> p (c n)")[:, off:off + n * mul]
        if dt != BF16:
            ap = ap.bitcast(dt)
        if len(shape) == 1:
            return ap
        names = " ".join("d%d" % i for i in range(len(shape)))
        kw = {"d%d" % i: shape[i] for i in range(len(shape) - 1)}
        return ap.rearrange("p (%s) -> p %s" % (names, names), **kw)

    def s5_views(self):
        Bm = self.bv(24576, [8, 4, 2, 64])
        Cm = self.bv(28672, [8, 4, 2, 64])
        return Bm, Cm

    def s5_coefs(self, P, W, off, lr, li, ldt, pB, want_coef):
        S = self.S
        parts = (0, P)
        n = W * 2
        names = ["dt", "mag", "th", "t1", "t2", "sin", "cos", "ar", "ai", "den", "cre", "cim", "nr"]
        V = {k: self.av(off + i * n, [W], F32, parts) for i, k in enumerate(names)}
        ti = self.av(off + len(names) * n, [W], mybir.dt.int32, parts)
        R, Wr = [pB], [pB]
        TWO_PI = 2.0 * math.pi

        def dve(fn):
            S.op("dve", fn, R, Wr)

        def act(fn):
            S.op("act", fn, R, Wr)

        act(lambda E: E.activation(out=V["dt"], in_=ldt, func=AF.Exp))
        dve(lambda E: E.tensor_tensor(out=V["mag"], in0=lr, in1=V["dt"], op=ALU.mult))
        act(lambda E: E.activation(out=V["mag"], in_=V["mag"], func=AF.Exp))
        dve(lambda E: E.tensor_tensor(out=V["th"], in0=li, in1=V["dt"], op=ALU.mult))
        for name, shift in (("sin", 0.0), ("cos", 0.5 * math.pi)):
            o = V[name]
            dve(lambda E, shift=shift: E.tensor_scalar(out=V["t2"], in0=V["th"], scalar1=shift, scalar2=None,
                                                        op0=ALU.add))
            dve(lambda E: E.tensor_scalar(out=V["t1"], in0=V["t2"], scalar1=1.0 / TWO_PI, scalar2=None, op0=ALU.mult))
            dve(lambda E: E.tensor_copy(out=ti, in_=V["t1"]))
            dve(lambda E: E.tensor_copy(out=V["t1"], in_=ti))
            dve(lambda E: E.scalar_tensor_tensor(out=V["t1"], in0=V["t1"], scalar=-TWO_PI, in1=V["t2"], op0=ALU.mult,
                                                 op1=ALU.add))
            dve(lambda E: E.tensor_scalar(out=V["t2"], in0=V["t1"], scalar1=math.pi, scalar2=-TWO_PI, op0=ALU.is_gt,
                                          op1=ALU.mult))
            dve(lambda E: E.tensor_tensor(out=V["t1"], in0=V["t1"], in1=V["t2"], op=ALU.add))
            dve(lambda E: E.tensor_scalar(out=V["t2"], in0=V["t1"], scalar1=-math.pi, scalar2=TWO_PI, op0=ALU.is_lt,
                                          op1=ALU.mult))
            dve(lambda E: E.tensor_tensor(out=V["t1"], in0=V["t1"], in1=V["t2"], op=ALU.add))
            act(lambda E, o=o: E.activation(out=o, in_=V["t1"], func=AF.Sin))
        dve(lambda E: E.tensor_tensor(out=V["ar"], in0=V["mag"], in1=V["cos"], op=ALU.mult))
        dve(lambda E: E.tensor_tensor(out=V["ai"], in0=V["mag"], in1=V["sin"], op=ALU.mult))
        if want_coef:
            dve(lambda E: E.tensor_tensor(out=V["den"], in0=lr, in1=lr, op=ALU.mult))
            dve(lambda E: E.tensor_tensor(out=V["t1"], in0=li, in1=li, op=ALU.mult))
            dve(lambda E: E.tensor_tensor(out=V["den"], in0=V["den"], in1=V["t1"], op=ALU.add))
            dve(lambda E: E.reciprocal(out=V["t2"], in_=V["den"]))
            dve(lambda E: E.tensor_scalar(out=V["nr"], in0=V["ar"], scalar1=-1.0, scalar2=None, op0=ALU.add))
            dve(lambda E: E.tensor_tensor(out=V["cre"], in0=V["nr"], in1=lr, op=ALU.mult))
            dve(lambda E: E.tensor_tensor(out=V["t1"], in0=V["ai"], in1=li, op=ALU.mult))
            dve(lambda E: E.tensor_tensor(out=V["cre"], in0=V["cre"], in1=V["t1"], op=ALU.add))
            dve(lambda E: E.tensor_tensor(out=V["cre"], in0=V["cre"], in1=V["t2"], op=ALU.mult))
            dve(lambda E: E.tensor_tensor(out=V["cim"], in0=V["ai"], in1=lr, op=ALU.mult))
            dve(lambda E: E.tensor_tensor(out=V["t1"], in0=V["nr"], in1=li, op=ALU.mult))
            dve(lambda E: E.tensor_tensor(out=V["cim"], in0=V["cim"], in1=V["t1"], op=ALU.subtract))
            dve(lambda E: E.tensor_tensor(out=V["cim"], in0=V["cim"], in1=V["t2"], op=ALU.mult))
        return V

    def s5_prologue(self, j):
        S = self.S
        S.fence()
        pB = Buf("s5pro")
        Bm, Cm = self.s5_views()
        BmB, CmB = self.s5B
        w = self.w
        lr = self.av(0, [32], F32); li = self.av(64, [32], F32); ldt = self.av(128, [32], F32)
        for m in range(2):
            for src, dst in ((w["s5_lam_re"][j], lr), (w["s5_lam_im"][j], li)):
                for q in range(4):
                    S.dma("sp", "misc", dst[64 * m:64 * m + 64, :].rearrange("p (a b) -> p a b", a=8)[:, :, q],
                          src.rearrange("(cc m q) n -> m q n cc", m=2, q=4)[m, q], writes=[pB], slow=True)
            S.dma("sp", "misc", ldt[64 * m:64 * m + 64, :].rearrange("p (a b) -> p a b", a=8),
                  w["s5_log_dt"][j].rearrange("(cc m q) -> m cc q", m=2, q=4)[m].partition_broadcast(64),
                  writes=[pB], slow=True)
        V = self.s5_coefs(128, 32, 192, lr, li, ldt, pB, False)
        Ac = self.Acoef
        acB = self.acB
        S.op("dve", lambda E: E.tensor_copy(out=Ac[:, 0, 0, :], in_=V["ar"]), [pB], [acB])
        S.op("dve", lambda E: E.tensor_copy(out=Ac[:, 1, 1, :], in_=V["ar"]), [pB], [acB])
        S.op("dve", lambda E: E.tensor_copy(out=Ac[:, 1, 0, :], in_=V["ai"]), [pB], [acB])
        S.op("dve", lambda E: E.tensor_scalar(out=Ac[:, 0, 1, :], in0=V["ai"], scalar1=-1.0, scalar2=None,
                                              op0=ALU.mult), [pB], [acB])
        rm8 = self.av(1600, [8], F32)
        S.op("pool", lambda E: E.memset(rm8, 1.0), [pB], [pB])
        S.op("pool", lambda E: E.affine_select(out=rm8, in_=rm8, pattern=[[-16, 8]], compare_op=ALU.is_ge, fill=0.0,
                                               base=0, channel_multiplier=1), [pB], [pB])
        S.op("pool", lambda E: E.affine_select(out=rm8, in_=rm8, pattern=[[16, 8]], compare_op=ALU.is_ge, fill=0.0,
                                               base=15, channel_multiplier=-1), [pB], [pB])
        rowm = self.rowm
        S.op("dve", lambda E: E.tensor_tensor(out=rowm[:], in0=rm8[:, 0:4], in1=rm8[:, 4:8], op=ALU.add), [pB], [pB])
        colm = self.colm
        S.op("pool", lambda E: E.memset(colm[:], 0.0), [pB], [pB])
        for q in range(4):
            S.op("pool", lambda E, q=q: E.memset(colm[:, q, 16 * q:16 * q + 16], 1.0), [pB], [pB])
        dg = self.diagD
        for cc in range(8):
            S.op("dve", lambda E, cc=cc: E.tensor_scalar(out=dg[:, cc, :], in0=self.ident[:],
                                                         scalar1=self.gain("s5d%d" % j, cc), scalar2=None,
                                                         op0=ALU.mult), [self.cB, pB], [pB])
        Cin = [self.av(4096 + 2048 * r, [8, 2, 64], F32, (0, 64)) for r in range(2)]
        for r, nm in enumerate(("s5_c_re", "s5_c_im")):
            S.dma("sp", "misc", Cin[r], w[nm][j].rearrange("(cc m q) c n -> (q c) cc m n", m=2, q=4), writes=[pB])
        for r in range(2):
            for cc in range(8):
                ps, pb = self.bank()
                S.op("pe", lambda E, ps=ps, r=r, cc=cc: E.transpose(
                    out=ps[:, 0:64], in_=Cin[r][:, cc, :, :].rearrange("p m n -> p (m n)"),
                    identity=self.ident[0:64, 0:64]), [pB, self.cB], [pb])
                for q in range(4):
                    S.op("dve", lambda E, ps=ps, r=r, cc=cc, q=q: E.scalar_tensor_tensor(
                        out=Cm[:, cc, q, r, :], in0=ps[:, 0:64], scalar=(1.0 if r == 0 else -1.0), in1=colm[:, q, :],
                        op0=ALU.mult, op1=ALU.mult), [pb, pB], [CmB])
        S.fence()
        lrn = self.av(0, [64], F32, (0, 64)); lin = self.av(128, [64], F32, (0, 64)); ldn = self.av(256, [64], F32, (0, 64))
        S.dma("sp", "misc", lrn, w["s5_lam_re"][j].rearrange("g n -> n g"), writes=[pB], slow=True)
        S.dma("sp", "misc", lin, w["s5_lam_im"][j].rearrange("g n -> n g"), writes=[pB], slow=True)
        S.dma("sp", "misc", ldn, w["s5_log_dt"][j].partition_broadcast(64), writes=[pB], slow=True)
        Vn = self.s5_coefs(64, 64, 384, lrn, lin, ldn, pB, True)
        bre = self.av(4096, [64, 16], F32, (0, 64)); bim = self.av(6144, [64, 16], F32, (0, 64))
        bbr = self.av(8192, [64, 16], F32, (0, 64)); bbi = self.av(10240, [64, 16], F32, (0, 64))
        tmpb = self.av(12288, [64, 16], F32, (0, 64))
        S.dma("sp", "misc", bre, w["s5_b_re"][j].rearrange("g n c -> n g c"), writes=[pB])
        S.dma("sp", "misc", bim, w["s5_b_im"][j].rearrange("g n c -> n g c"), writes=[pB])
        cre_b = Vn["cre"].unsqueeze(2).to_broadcast([64, 64, 16])
        cim_b = Vn["cim"].unsqueeze(2).to_broadcast([64, 64, 16])
        S.op("dve", lambda E: E.tensor_tensor(out=bbr, in0=bre, in1=cre_b, op=ALU.mult), [pB], [pB])
        S.op("dve", lambda E: E.tensor_tensor(out=tmpb, in0=bim, in1=cim_b, op=ALU.mult), [pB], [pB])
        S.op("dve", lambda E: E.tensor_tensor(out=bbr, in0=bbr, in1=tmpb, op=ALU.subtract), [pB], [pB])
        S.op("dve", lambda E: E.tensor_tensor(out=bbi, in0=bim, in1=cre_b, op=ALU.mult), [pB], [pB])
        S.op("dve", lambda E: E.tensor_tensor(out=tmpb, in0=bre, in1=cim_b, op=ALU.mult), [pB], [pB])
        S.op("dve", lambda E: E.tensor_tensor(out=bbi, in0=bbi, in1=tmpb, op=ALU.add), [pB], [pB])
        for r, bb in enumerate((bbr, bbi)):
            for cc in range(8):
                ps, pb = self.bank()
                S.op("pe", lambda E, ps=ps, bb=bb, cc=cc: E.transpose(
                    out=ps[:, 0:64], in_=bb[:, 8 * cc:8 * cc + 8, :].rearrange("p g c -> p (g c)"),
                    identity=self.ident[0:64, 0:64]), [pB, self.cB], [pb])
                for q in range(4):
                    S.op("dve", lambda E, ps=ps, r=r, cc=cc, q=q: E.tensor_scalar(
                        out=Bm[:, cc, q, r, :], in0=ps[:, 0:64], scalar1=rowm[:, q:q + 1], scalar2=None, op0=ALU.mult),
                        [pb, pB], [BmB])
        S.fence()

    def s5(self, j, s):
        S = self.S
        NT = 32
        NCH = T // NT
        Bm, Cm = self.s5_views()
        BmB, CmB = self.s5B
        uT = self.bv(0, [8, T])
        uB = Buf("uT")
        Xc = [self.bv(16384 + i * 2048, [NT, 2, 32]) for i in range(2)]
        Hc = [self.bv(20480 + i * 2048, [NT, 2, 32]) for i in range(2)]
        XB = [Buf(), Buf()]
        HB = [Buf(), Buf()]
        ttB, u2B = Buf(), Buf()
        hsB = [Buf(), Buf()]
        Ac, Tt, U2, Hs = self.Acoef, self.Tt, self.U2, self.Hs
        Win = self.w["s5_w_in"][j]
        for half in range(2):
            w, wb = self.load_w(self.wview(Win, half * 512, 512), (8, 512))
            for tt in range(4):
                for m in range(4):
                    ps, pb = self.proj_fm(w, wb, m, tt)
                    self.evac(uT[:, half * 4 + m, tt * 512:(tt + 1) * 512], ps, [pb], [uB])
        S.fence()
        yB = Buf("yT")
        tt2B, u22B = [Buf(), Buf()], [Buf(), Buf()]
        hs2B = [[Buf(), Buf()], [Buf(), Buf()]]
        S.op("pool", lambda E: E.memset(Hs[0][:], 0.0), [hs2B[0][0], hs2B[0][1]], [hs2B[0][0], hs2B[0][1]])

        def emit_x(tc):
            b = tc % 2
            t0 = tc * NT
            for g4 in range(8):
                ps, pb = self.bank()
                for q in range(4):
                    for m in range(2):
                        for r in range(2):
                            col = (q * 2 + r) * NT
                            S.op("pe", lambda E, ps=ps, g4=g4, q=q, m=m, r=r, col=col, t0=t0: E.matmul(
                                ps[64 * m:64 * m + 64, col:col + NT], Bm[64 * m:64 * m + 64, g4, q, r, :],
                                uT[64 * m:64 * m + 64, g4, t0:t0 + NT], start=True, stop=True), [BmB, uB], [pb],
                                inc=(q == 3 and m == 1 and r == 1))
                dst = Xc[b][:, :, :, 4 * g4:4 * g4 + 4].rearrange("p t r q -> p q r t")
                src = ps[:, 0:8 * NT].rearrange("p (q r t) -> p q r t", q=4, r=2)
                S.op("act", lambda E, dst=dst, src=src: E.activation(out=dst, in_=src, func=AF.Copy), [pb], [XB[b]])

        def emit_scan(tc):
            b = tc % 2
            HS = (slice(0, 16), slice(16, 32))
            for i in range(NT):
                t = tc * NT + i
                prev, nxt = Hs[t % 2], Hs[(t + 1) % 2]
                pvB, nxB = hs2B[t % 2], hs2B[(t + 1) % 2]
                for hf in range(2):
                    S.op("dve", lambda E, prev=prev, hf=hf: E.tensor_tensor(
                        out=Tt[:, :, :, HS[hf]], in0=Ac[:, :, :, HS[hf]],
                        in1=prev[:, :, HS[hf]].unsqueeze(1).to_broadcast([128, 2, 2, 16]), op=ALU.mult),
                        [self.acB, pvB[hf]], [tt2B[hf]])
                for hf in range(2):
                    S.op("dve", lambda E, b=b, i=i, hf=hf: E.tensor_tensor(
                        out=U2[:, :, HS[hf]], in0=Tt[:, :, 0, HS[hf]], in1=Xc[b][:, i, :, HS[hf]], op=ALU.add),
                        [tt2B[hf], XB[b]], [u22B[hf]])
                for hf in range(2):
                    S.op("dve", lambda E, nxt=nxt, hf=hf: E.tensor_tensor(
                        out=nxt[:, :, HS[hf]], in0=U2[:, :, HS[hf]], in1=Tt[:, :, 1, HS[hf]], op=ALU.add),
                        [u22B[hf], tt2B[hf]], [nxB[hf]])
                S.op("act", lambda E, nxt=nxt, b=b, i=i: E.activation(out=Hc[b][:, i, :, :], in_=nxt[:], func=AF.Copy),
                     [nxB[0], nxB[1]], [HB[b]])

        def emit_y(tc):
            b = tc % 2
            t0 = tc * NT
            for half in range(2):
                ps, pb = self.bank()
                for c4 in range(4):
                    cc = half * 4 + c4
                    col = c4 * NT
                    S.op("pe", lambda E, ps=ps, cc=cc, col=col, t0=t0: E.matmul(
                        ps[:, col:col + NT], self.diagD[:, cc, :], uT[:, cc, t0:t0 + NT], start=True, stop=False),
                        [self.cB, uB], [pb], inc=False)
                    for m in range(2):
                        for q in range(4):
                            for r in range(2):
                                S.op("pe", lambda E, ps=ps, cc=cc, col=col, m=m, q=q, r=r, b=b: E.matmul(
                                    ps[64 * m:64 * m + 64, col:col + NT], Cm[64 * m:64 * m + 64, cc, q, r, :],
                                    Hc[b][64 * m:64 * m + 64, :, r, 4 * cc + q], start=False,
                                    stop=(q == 3 and r == 1)), [CmB, HB[b]], [pb],
                                    inc=(c4 == 3 and m == 1 and q == 3 and r == 1))
                dst = self.A[:, half * 4:half * 4 + 4, t0:t0 + NT]
                src = ps[:, 0:4 * NT].rearrange("p (c t) -> p c t", c=4)
                S.op("act", lambda E, dst=dst, src=src: E.activation(out=dst, in_=src, func=AF.Gelu_apprx_tanh),
                     [pb], [yB])

        emit_x(0)
        for tc in range(NCH):
            if tc + 1 < NCH:
                emit_x(tc + 1)
            emit_scan(tc)
            emit_y(tc)
        S.fence()
        gT = uT
        gB = Buf("gT")
        Wg3 = self.w["s5_w_glu"][j].rearrange("(kc p) n -> p kc n", p=128)
        for oc in range(8):
            wz, wzb = self.load_w2([(Wg3[:, :, 128 * oc:128 * oc + 128], 128),
                                    (Wg3[:, :, 1024 + 128 * oc:1024 + 128 * oc + 128], 128)], (8, 256))
            for tt in range(4):
                ps1, pb1 = self.proj_fm(wz, wzb, 0, tt, srcB=yB)
                ps2, pb2 = self.proj_fm(wz, wzb, 1, tt, srcB=yB)
                sg, sgb = self.tmp32()
                S.op("act", lambda E, sg=sg, ps2=ps2: E.activation(out=sg[:], in_=ps2, func=AF.Sigmoid), [pb2], [sgb])
                S.op("dve", lambda E, sg=sg, ps1=ps1, oc=oc, tt=tt: E.tensor_tensor(
                    out=gT[:, oc, tt * 512:(tt + 1) * 512], in0=ps1, in1=sg[:], op=ALU.mult), [pb1, sgb], [gB])
        self.out_proj(self.w["s5_w_out"][j], 8, gT, gB)


_CACHE = {}


def get_prog(nseq, **kw):
    key = (nseq, tuple(sorted(kw.items())))
    if key not in _CACHE:
        _CACHE[key] = Prog(nseq, **kw)
    return _CACHE[key]


def kernel(**inputs):
    nseq = 16 // NCORES
    prog = get_prog(nseq)
    x = np.ascontiguousarray(inputs["x"], dtype=np.float32)
    p = np.ascontiguousarray(inputs["p"], dtype=np.float32)
    in_maps = []
    for c in range(NCORES):
        m = {"x": np.ascontiguousarray(x[c * nseq:(c + 1) * nseq]),
             "p": np.ascontiguousarray(p[:, c * nseq:(c + 1) * nseq])}
        for name, shape in WSPEC:
            m[name] = np.ascontiguousarray(inputs[name], dtype=np.float32)
        in_maps.append(m)
    res = run_bass_kernel_spmd(prog.nc, in_maps, core_ids=list(range(NCORES)))
    return np.concatenate([r["out"] for r in res.results], axis=0)
```

```python
import contextlib
import math
import numpy as np
import concourse.bass as bass
import concourse.mybir as mybir
from concourse.bass_utils import run_bass_kernel_spmd

F32 = mybir.dt.float32
BF16 = mybir.dt.bfloat16
AF = mybir.ActivationFunctionType
ALU = mybir.AluOpType

D = 1024
T = 2048
DEPTH = 4
NCORES = 8
EPS = 1e-6


class Clock:
    __slots__ = ("sem", "count", "step", "name")

    def __init__(self, sem, step, name):
        self.sem = sem
        self.count = 0
        self.step = step
        self.name = name


class Buf:
    __slots__ = ("name", "w", "r")

    def __init__(self, name=""):
        self.name = name
        self.w = None
        self.r = {}


class Eng:
    def __init__(self, name, clock):
        self.name = name
        self.clock = clock
        self.ops = []
        self.seen = {}


class Sched:
    def __init__(self, nc, stack):
        self.nc = nc
        self.stack = stack
        self.eng = {}
        for n in ("pe", "act", "dve", "pool", "sp"):
            sem = stack.enter_context(nc.semaphore("clk_" + n))
            self.eng[n] = Eng(n, Clock(sem, 1, n))
        self.dclk = {}

    def dma_clock(self, key):
        if key not in self.dclk:
            sem = self.stack.enter_context(self.nc.semaphore("dq_" + str(key)))
            self.dclk[key] = Clock(sem, 16, "dma_" + str(key))
        return self.dclk[key]

    def _need(self, e, reads, writes):
        need = {}

        def add(cv):
            if cv is None:
                return
            c, v = cv
            if need.get(c, 0) < v:
                need[c] = v

        for b in reads:
            add(b.w)
        for b in writes:
            add(b.w)
            for c, v in b.r.items():
                add((c, v))
        out = []
        for c, v in need.items():
            if c is e.clock:
                if e.name == "pe":
                    continue
                if v < c.count:
                    continue
            if e.seen.get(c, 0) >= v:
                continue
            e.seen[c] = v
            out.append((c, v))
        return out

    def op(self, en, fn, reads=(), writes=(), inc=True):
        e = self.eng[en]
        for c, v in self._need(e, reads, writes):
            e.ops.append(("w", c.sem, v))
        clk = e.clock
        val = clk.count + 1
        e.ops.append(("o", fn, clk.sem if inc else None, 1))
        for b in reads:
            if b.r.get(clk, 0) < val:
                b.r[clk] = val
        for b in writes:
            b.w = (clk, val)
            b.r = {}
        if inc:
            clk.count = val

    def dma(self, qn, key, out, in_, reads=(), writes=(), slow=False, batch=False):
        e = self.eng[qn]
        for c, v in self._need(e, reads, writes):
            e.ops.append(("w", c.sem, v))
        clk = self.dma_clock(key)
        if not batch and clk.count > 0 and e.seen.get(clk, 0) < clk.count:
            e.ops.append(("w", clk.sem, clk.count))
            e.seen[clk] = clk.count
        val = clk.count + 16
        if slow:
            e.ops.append(("o", (lambda E, o=out, i=in_: E.dma_start(out=o, in_=i, allow_slow_non_contiguous=True)),
                          clk.sem, 16))
        else:
            e.ops.append(("o", (lambda E, o=out, i=in_: E.dma_start(out=o, in_=i)), clk.sem, 16))
        for b in reads:
            if b.r.get(clk, 0) < val:
                b.r[clk] = val
        for b in writes:
            b.w = (clk, val)
            b.r = {}
        clk.count = val

    def fence(self):
        clocks = [e.clock for e in self.eng.values()] + list(self.dclk.values())
        for e in self.eng.values():
            for c in clocks:
                if c is e.clock or c.count == 0:
                    continue
                if e.seen.get(c, 0) < c.count:
                    e.ops.append(("w", c.sem, c.count))
                    e.seen[c] = c.count

    def emit(self):
        nc = self.nc
        with nc.Block() as block:
            def mk(en):
                ops = self.eng[en].ops

                def body(E):
                    for o in ops:
                        if o[0] == "w":
                            E.wait_ge(o[1], o[2])
                        else:
                            ins = o[1](E)
                            if o[2] is not None:
                                ins.then_inc(o[2], o[3])
                return body
            block.tensor(mk("pe"))
            block.scalar(mk("act"))
            block.vector(mk("dve"))
            block.gpsimd(mk("pool"))
            block.sync(mk("sp"))


WSPEC = [
    ("norm_mix", [4, D]), ("norm_mlp", [4, D]), ("norm_ple", [4, D]),
    ("s5_w_in", [2, D, D]), ("s5_lam_re", [2, 64, 64]), ("s5_lam_im", [2, 64, 64]),
    ("s5_log_dt", [2, 64]), ("s5_b_re", [2, 64, 64, 16]), ("s5_b_im", [2, 64, 64, 16]),
    ("s5_c_re", [2, 64, 16, 64]), ("s5_c_im", [2, 64, 16, 64]), ("s5_d", [2, D]),
    ("s5_w_glu", [2, D, 2 * D]), ("s5_w_out", [2, D, D]),
    ("fox_w_in", [1, D, 3088]), ("fox_b_f", [1, 16]), ("fox_w_out", [1, D, D]),
    ("gla_w_in", [1, D, 3088]), ("gla_w_g2", [1, 16, 512]), ("gla_b_g", [1, 512]),
    ("gla_norm", [1, D]), ("gla_w_out", [1, D, D]),
    ("mlp_w1", [4, D, 4 * D]), ("mlp_w2", [4, 4 * D, D]),
    ("ple_proj", [4, 256, D]), ("ple_gate", [4, D, D]), ("final_norm", [D]),
]


class Prog:
    def __init__(self, nseq, layers=(0, 1, 2, 3), mixers=True, mlp=True, ple=True, dbg=0):
        self.nseq = nseq
        self.dbg = dbg
        self.layers = layers
        self.mixers = mixers
        self.do_mlp = mlp
        self.do_ple = ple
        self.nc = bass.Bass("TRN2", target_bir_lowering=False)
        nc = self.nc
        self.x = nc.dram_tensor("x", [nseq, T, D], F32, kind="ExternalInput").ap()
        self.p = nc.dram_tensor("p", [DEPTH, nseq, T, 256], F32, kind="ExternalInput").ap()
        self.w = {}
        for name, shape in WSPEC:
            self.w[name] = nc.dram_tensor(name, shape, F32, kind="ExternalInput").ap()
        self.out = nc.dram_tensor("out", [nseq, T, D], F32, kind="ExternalOutput").ap()
        with contextlib.ExitStack() as st:
            self.st = st
            self.S = Sched(nc, st)
            self.alloc()
            self.consts()
            for s in range(nseq):
                self.sequence(s)
            S = self.S
            S.fence()
            S.emit()

    def tile(self, name, shape, dt=F32):
        return self.st.enter_context(self.nc.sbuf_tensor(name, shape, dt))

    def alloc(self):
        nc, st = self.nc, self.st
        self.hT = self.tile("hT", [128, 8, T], F32)
        self.hB = [[Buf("h%d_%d" % (c, tt)) for tt in range(4)] for c in range(8)]
        self.A = self.tile("A", [128, 8, T], BF16)
        self.AB = [[Buf("A%d_%d" % (c, tt)) for tt in range(4)] for c in range(8)]
        self.rt2 = self.tile("rt2", [128, 512], F32)
        self.rt2B = Buf("rt2")
        self.rt_i = 0
        self.Alo = self.tile("Alo", [128, 8, 256], BF16)
        self.AloB = [Buf("Alo%d" % c) for c in range(8)]
        self.NSLOT = 3
        self.ring = [self.tile("ring%d" % i, [128, 4096], BF16) for i in range(self.NSLOT)]
        self.ringB = [Buf("ring%d" % i) for i in range(self.NSLOT)]
        self.ring_i = 0
        self.psA = st.enter_context(nc.psum_tensor("psA", [128, 6, 512], F32))
        self.psAB = [Buf("psA%d" % i) for i in range(6)]
        self.NROT = 4
        self.psA_i = 0
        self.psT = st.enter_context(nc.psum_tensor("psT", [128, 2, 1024], BF16))
        self.psTB = [Buf("psT%d" % i) for i in range(2)]
        self.psT_i = 0
        self.sq = [self.tile("sq%d" % i, [128, 512], BF16) for i in range(2)]
        self.sqB = [Buf() for _ in range(2)]
        self.sq_i = 0
        self.f32t = [self.tile("f32t%d" % i, [128, 512], F32) for i in range(4)]
        self.f32tB = [Buf() for _ in range(4)]
        self.f32t_i = 0
        self.B = self.tile("B", [128, 32768], BF16)
        self.aT = [self.bv(i * 2048, [4, 512]) for i in range(2)]
        self.aTB = [Buf() for _ in range(2)]
        self.stage = [self.bv(i * 2048, [D], F32) for i in range(2)]
        self.stageB = [Buf() for _ in range(2)]
        self.pstage = self.bv(8192, [16, 256])
        self.pstageB = Buf()
        self.pT = self.bv(12288, [2, T])
        self.pTB = Buf()
        self.evac_i = 0
        self.Acoef = self.tile("Acoef", [128, 2, 2, 32], F32)
        self.Tt = self.tile("Tt", [128, 2, 2, 32], F32)
        self.U2 = self.tile("U2", [128, 2, 32], F32)
        self.Hs = [self.tile("Hs%d" % i, [128, 2, 32], F32) for i in range(2)]
        self.acB = Buf("Acoef")
        self.s5B = (Buf("Bm"), Buf("Cm"))
        self.diagD = self.tile("diagD", [128, 8, 128], BF16)
        self.rowm = self.tile("rowm", [128, 4], F32)
        self.colm = self.tile("colm", [128, 4, 64], F32)
        print("sbuf bytes remaining", nc.sbuf_bytes_remaining)

    def bv(self, off, shape, dt=BF16, parts=(0, 128)):
        n = int(np.prod(shape))
        mul = 2 if dt == F32 else 1
        ap = self.B[parts[0]:parts[1], off:off + n * mul]
        if dt == F32:
            ap = ap.bitcast(F32)
        if len(shape) == 1:
            return ap
        names = " ".join("d%d" % i for i in range(len(shape)))
        kw = {"d%d" % i: shape[i] for i in range(len(shape) - 1)}
        return ap.rearrange("p (%s) -> p %s" % (names, names), **kw)

    def bank(self):
        i = self.psA_i
        self.psA_i = (i + 1) % self.NROT
        return self.psA[:, i, :], self.psAB[i]

    def dbank(self, k):
        return self.psA[:, 4 + k, :], self.psAB[4 + k]

    def tbank(self):
        i = self.psT_i
        self.psT_i = (i + 1) % 2
        return self.psT[:, i, :], self.psTB[i]

    def tmp32(self):
        i = self.f32t_i
        self.f32t_i = (i + 1) % 3
        return self.f32t[i], self.f32tB[i]

    def evac_eng(self):
        self.evac_i ^= 1
        return "act" if self.evac_i else "dve"

    def load_w(self, src3, shape):
        i = self.ring_i
        self.ring_i = (i + 1) % self.NSLOT
        a, b = shape
        assert a * b <= 4096
        view = self.ring[i][:, 0:a * b].rearrange("p (a b) -> p a b", a=a)
        self.S.dma("pool", "ring%d" % i, view, src3, writes=[self.ringB[i]])
        return view, self.ringB[i]

    def consts(self):
        S, nc = self.S, self.nc
        self.ident = self.tile("ident", [128, 128], F32)
        self.identb = self.tile("identb", [128, 128], BF16)
        self.onesb = self.tile("onesb", [128, 128], BF16)

        self.epsT = self.tile("epsT", [128, 1], F32)
        self.cB = Buf("consts")
        cB = self.cB
        S.op("pool", lambda E: E.memset(self.ident[:], 1.0), [], [cB])
        S.op("pool", lambda E: E.affine_select(out=self.ident[:], in_=self.ident[:], pattern=[[-1, 128]],
                                               compare_op=ALU.is_equal, fill=0.0, base=0,
                                               channel_multiplier=1), [cB], [cB])
        S.op("pool", lambda E: E.memset(self.onesb[:], 1.0), [], [cB])

        S.op("pool", lambda E: E.memset(self.epsT[:], EPS), [], [cB])
        S.op("dve", lambda E: E.tensor_copy(out=self.identb[:], in_=self.ident[:]), [cB], [cB])
        vecs = []
        for i in range(4):
            vecs.append(("mix%d" % i, self.w["norm_mix"][i]))
            vecs.append(("mlp%d" % i, self.w["norm_mlp"][i]))
            vecs.append(("ple%d" % i, self.w["norm_ple"][i]))
        vecs.append(("final", self.w["final_norm"]))
        vecs.append(("gla_norm", self.w["gla_norm"][0]))
        vecs.append(("s5d0", self.w["s5_d"][0]))
        vecs.append(("s5d1", self.w["s5_d"][1]))
        self.gidx = {n: i for i, (n, _) in enumerate(vecs)}
        self.gains = self.tile("gains", [128, len(vecs), 8], F32)
        for i, (n, ap) in enumerate(vecs):
            S.dma("sp", "misc", self.gains[:, i, :], ap.rearrange("(c p) -> p c", p=128), writes=[cB], slow=True)
        S.fence()

    def gain(self, name, c):
        return self.gains[:, self.gidx[name], c:c + 1]

    def load_x(self, s):
        S = self.S
        for tb in range(16):
            stg, sb = self.stage[tb % 2], self.stageB[tb % 2]
            S.dma("sp", "stage%d" % (tb % 2), stg[:], self.x[s, tb * 128:(tb + 1) * 128, :], writes=[sb])
            for half in range(2):
                ps, pb = self.bank()
                for q in range(4):
                    c = half * 4 + q
                    S.op("pe", lambda E, ps=ps, q=q, c=c, stg=stg: E.transpose(
                        out=ps[:, q * 128:(q + 1) * 128], in_=stg[:, c * 128:(c + 1) * 128],
                        identity=self.ident[:]), [sb, self.cB], [pb], inc=(q == 3))
                dst = self.hT[:, half * 4:half * 4 + 4, tb * 128:(tb + 1) * 128]
                src = ps.rearrange("p (q t) -> p q t", q=4)
                wb = [self.hB[half * 4 + q][tb // 4] for q in range(4)]
                if self.evac_eng() == "act":
                    S.op("act", lambda E, dst=dst, src=src: E.activation(out=dst, in_=src, func=AF.Copy), [pb], wb)
                else:
                    S.op("dve", lambda E, dst=dst, src=src: E.tensor_copy(out=dst, in_=src), [pb], wb)

    def rstd_tile(self, tt, src_tile, srcB, nchunks, chunk0=0, scale=1.0 / D):
        S = self.S
        ps, pb = self.bank()
        for k in range(nchunks):
            c = chunk0 + k
            i = self.sq_i
            self.sq_i ^= 1
            sq, sqb = self.sq[i], self.sqB[i]
            S.op("act", lambda E, sq=sq, c=c: E.activation(
                out=sq[:], in_=src_tile[:, c, tt * 512:(tt + 1) * 512], func=AF.Square), [srcB[c][tt]], [sqb])
            S.op("pe", lambda E, ps=ps, sq=sq, k=k: E.matmul(ps, self.onesb[:], sq[:], start=(k == 0),
                                                            stop=(k == nchunks - 1)),
                 [sqb, self.cB], [pb])
        self.rt_i ^= 1
        rt, rtb = (self.f32t[3], self.f32tB[3]) if self.rt_i else (self.rt2, self.rt2B)
        S.op("act", lambda E, rt=rt, ps=ps: E.activation(out=rt[:], in_=ps, func=AF.Sqrt, bias=self.epsT[:],
                                                         scale=scale), [pb, self.cB], [rtb])
        S.op("dve", lambda E, rt=rt: E.reciprocal(out=rt[:], in_=rt[:]), [rtb], [rtb])
        return rt, rtb

    def rmsnorm_to_A(self, gname):
        S = self.S
        k = 0
        for tt in range(4):
            rt, rtb = self.rstd_tile(tt, self.hT, self.hB, 8)
            for c in range(8):
                en = "dve"
                if tt == 0:
                    t32, tb = self.tmp32()
                    S.op("dve", lambda E, c=c, rt=rt, t32=t32: E.scalar_tensor_tensor(
                        out=t32[:], in0=self.hT[:, c, 0:512], scalar=self.gain(gname, c), in1=rt[:], op0=ALU.mult,
                        op1=ALU.mult), [self.hB[c][0], rtb, self.cB], [tb])
                    S.op("act", lambda E, c=c, t32=t32: E.activation(out=self.A[:, c, 0:512], in_=t32[:], func=AF.Copy),
                         [tb], [self.AB[c][0]])
                    S.op("dve", lambda E, c=c, t32=t32: E.tensor_tensor(out=self.Alo[:, c, :], in0=t32[:, 0:256],
                                                                      in1=self.A[:, c, 0:256], op=ALU.subtract),
                         [tb, self.AB[c][0]], [self.AloB[c]])
                    continue
                S.op(en, lambda E, c=c, tt=tt, rt=rt: E.scalar_tensor_tensor(
                    out=self.A[:, c, tt * 512:(tt + 1) * 512], in0=self.hT[:, c, tt * 512:(tt + 1) * 512],
                    scalar=self.gain(gname, c), in1=rt[:], op0=ALU.mult, op1=ALU.mult),
                    [self.hB[c][tt], rtb, self.cB], [self.AB[c][tt]])

    def wview(self, W2d, c0, ncols):
        return W2d.rearrange("(kc p) n -> p kc n", p=128)[:, :, c0:c0 + ncols]

    def add_to_h(self, ps, pb, c, tt):
        S = self.S
        S.op("dve", lambda E, ps=ps, c=c, tt=tt: E.tensor_tensor(
            out=self.hT[:, c, tt * 512:(tt + 1) * 512], in0=self.hT[:, c, tt * 512:(tt + 1) * 512], in1=ps,
            op=ALU.add), [pb, self.hB[c][tt]], [self.hB[c][tt]])

    def mlp(self, li):
        S = self.S
        W1 = self.w["mlp_w1"][li]
        W2 = self.w["mlp_w2"][li]
        it = 0
        for sl in range(8):
            w1, w1b = self.load_w(self.wview(W1, sl * 512, 512), (8, 512))
            w2, w2b = self.load_w(W2[sl * 512:(sl + 1) * 512, :].rearrange("(kc p) n -> p kc n", p=128), (4, 1024))
            for tt in range(4):
                aT, aTb = self.aT[it % 2], self.aTB[it % 2]
                it += 1
                for m in range(4):
                    ps, pb = self.bank()
                    for kc in range(8):
                        last = (kc == 7) and tt != 0
                        S.op("pe", lambda E, ps=ps, w1=w1, kc=kc, m=m, tt=tt, last=last: E.matmul(
                            ps, w1[:, kc, m * 128:(m + 1) * 128], self.A[:, kc, tt * 512:(tt + 1) * 512],
                            start=(kc == 0), stop=last), [w1b, self.AB[kc][tt]], [pb], inc=last)
                    if tt == 0:
                        for kc in range(8):
                            S.op("pe", lambda E, ps=ps, w1=w1, kc=kc, m=m: E.matmul(
                                ps[:, 0:256], w1[:, kc, m * 128:(m + 1) * 128], self.Alo[:, kc, :], start=False, stop=(kc == 7)),
                                [w1b, self.AloB[kc]], [pb], inc=(kc == 7))
                    r, rb = self.tmp32()
                    S.op("act", lambda E, r=r, ps=ps: E.activation(out=r[:], in_=ps, func=AF.Relu), [pb], [rb])
                    S.op("pool", lambda E, r=r, aT=aT, m=m: E.tensor_tensor(out=aT[:, m, :], in0=r[:], in1=r[:],
                                                                          op=ALU.mult), [rb], [aTb])
                for oc in range(8):
                    ps, pb = self.bank()
                    for m in range(4):
                        S.op("pe", lambda E, ps=ps, w2=w2, m=m, oc=oc, aT=aT: E.matmul(
                            ps, w2[:, m, oc * 128:(oc + 1) * 128], aT[:, m, :], start=(m == 0), stop=(m == 3)),
                            [w2b, aTb], [pb], inc=(m == 3))
                    self.add_to_h(ps, pb, oc, tt)

    def load_pT(self, li, s):
        S = self.S
        src = self.p[li, s].rearrange("(blk p) n -> p blk n", p=128)
        S.dma("pool", "pstage", self.pstage[:], src, writes=[self.pstageB])
        for fc in range(2):
            for g in range(2):
                ps, pb = self.tbank()
                for q in range(8):
                    blk = g * 8 + q
                    S.op("pe", lambda E, ps=ps, q=q, blk=blk, fc=fc: E.transpose(
                        out=ps[:, q * 128:(q + 1) * 128], in_=self.pstage[:, blk, fc * 128:(fc + 1) * 128],
                        identity=self.identb[:]), [self.pstageB, self.cB], [pb], inc=(q == 7))
                dst = self.pT[:, fc, g * 1024:(g + 1) * 1024]
                if self.evac_eng() == "act":
                    S.op("act", lambda E, dst=dst, ps=ps: E.activation(out=dst, in_=ps, func=AF.Copy), [pb], [self.pTB])
                else:
                    S.op("dve", lambda E, dst=dst, ps=ps: E.tensor_copy(out=dst, in_=ps), [pb], [self.pTB])

    def ple(self, li, s):
        S = self.S
        self.load_pT(li, s)
        Wg = self.w["ple_gate"][li]
        Wp = self.w["ple_proj"][li]
        for sl in range(2):
            wg, wgb = self.load_w(self.wview(Wg, sl * 512, 512), (8, 512))
            wp, wpb = self.load_w(self.wview(Wp, sl * 512, 512), (2, 512))
            for tt in range(4):
                for m in range(4):
                    oc = sl * 4 + m
                    ps, pb = self.bank()
                    for kc in range(8):
                        last = (kc == 7) and tt != 0
                        S.op("pe", lambda E, ps=ps, wg=wg, kc=kc, m=m, tt=tt, last=last: E.matmul(
                            ps, wg[:, kc, m * 128:(m + 1) * 128], self.A[:, kc, tt * 512:(tt + 1) * 512],
                            start=(kc == 0), stop=last), [wgb, self.AB[kc][tt]], [pb], inc=last)
                    if tt == 0:
                        for kc in range(8):
                            S.op("pe", lambda E, ps=ps, wg=wg, kc=kc, m=m: E.matmul(
                                ps[:, 0:256], wg[:, kc, m * 128:(m + 1) * 128], self.Alo[:, kc, :], start=False, stop=(kc == 7)),
                                [wgb, self.AloB[kc]], [pb], inc=(kc == 7))
                    g, gb = self.tmp32()
                    S.op("act", lambda E, g=g, ps=ps: E.activation(out=g[:], in_=ps, func=AF.Sigmoid), [pb], [gb])
                    ps2, pb2 = self.bank()
                    for kc in range(2):
                        S.op("pe", lambda E, ps2=ps2, wp=wp, kc=kc, m=m, tt=tt: E.matmul(
                            ps2, wp[:, kc, m * 128:(m + 1) * 128], self.pT[:, kc, tt * 512:(tt + 1) * 512],
                            start=(kc == 0), stop=(kc == 1)), [wpb, self.pTB], [pb2], inc=(kc == 1))
                    S.op("dve", lambda E, g=g, ps2=ps2: E.tensor_tensor(out=g[:], in0=g[:], in1=ps2, op=ALU.mult),
                         [gb, pb2], [gb])
                    S.op("pool", lambda E, g=g, oc=oc, tt=tt: E.tensor_tensor(
                        out=self.hT[:, oc, tt * 512:(tt + 1) * 512], in0=self.hT[:, oc, tt * 512:(tt + 1) * 512],
                        in1=g[:], op=ALU.add), [gb, self.hB[oc][tt]], [self.hB[oc][tt]])

    def final(self, s):
        S = self.S
        fA = self.A[:].rearrange("p c n -> p (c n)").bitcast(F32).rearrange("p (c n) -> p c n", c=8)
        for tt in range(4):
            rt, rtb = self.rstd_tile(tt, self.hT, self.hB, 8)
            half = tt % 2
            for c in range(8):
                S.op("dve", lambda E, c=c, tt=tt, rt=rt, half=half: E.scalar_tensor_tensor(
                    out=fA[:, c, half * 512:(half + 1) * 512], in0=self.hT[:, c, tt * 512:(tt + 1) * 512],
                    scalar=self.gain("final", c), in1=rt[:], op0=ALU.mult, op1=ALU.mult),
                    [self.hB[c][tt], rtb, self.cB], [self.AB[c][half]])
            for tbl in range(4):
                tb = tt * 4 + tbl
                stg, sb = self.stage[tb % 2], self.stageB[tb % 2]
                for hf in range(2):
                    ps, pb = self.bank()
                    for q in range(4):
                        c = hf * 4 + q
                        S.op("pe", lambda E, ps=ps, q=q, c=c, tbl=tbl, half=half: E.transpose(
                            out=ps[:, q * 128:(q + 1) * 128],
                            in_=fA[:, c, half * 512 + tbl * 128: half * 512 + (tbl + 1) * 128],
                            identity=self.ident[:]), [self.AB[c][half], self.cB], [pb], inc=(q == 3))
                    dst = stg[:, hf * 512:(hf + 1) * 512]
                    if self.evac_eng() == "act":
                        S.op("act", lambda E, dst=dst, ps=ps: E.activation(out=dst, in_=ps, func=AF.Copy), [pb], [sb])
                    else:
                        S.op("dve", lambda E, dst=dst, ps=ps: E.tensor_copy(out=dst, in_=ps), [pb], [sb])
                S.dma("sp", "ostage%d" % (tb % 2), self.out[s, tb * 128:(tb + 1) * 128, :], stg[:], reads=[sb])

    def sequence(self, s):
        S = self.S
        if self.dbg == 1:
            S.dma("sp", "dbg", self.out[s, 0:128, 0:8 * len(self.gidx)], self.gains[:].rearrange("p v c -> p (v c)"), reads=[self.cB])
            return
        self.load_x(s)
        if self.dbg == 3:
            S.fence()
            rt, rtb = self.rstd_tile(0, self.hT, self.hB, 8)
            S.dma("sp", "dbg", self.out[s, 0:128, 0:512], rt[:], reads=[rtb])
            return
        if self.dbg == 2:
            S.fence()
            for c in range(8):
                S.dma("sp", "dbg", self.out[s, c * 128:(c + 1) * 128, :], self.hT[:, c, 0:1024], reads=[self.hB[c][0], self.hB[c][1]])
            return
        for li in self.layers:
            if self.mixers:
                if li % 3 == 0:
                    self.s5_prologue(li // 3)
                self.rmsnorm_to_A("mix%d" % li)
                S.fence()
                m = li % 3
                if m == 0:
                    self.s5(li // 3, s)
                elif m == 1:
                    self.fox(li // 3, s)
                else:
                    self.gla(li // 3, s)
                S.fence()
            if self.do_mlp:
                self.rmsnorm_to_A("mlp%d" % li)
                self.mlp(li)
            if self.do_ple:
                self.rmsnorm_to_A("ple%d" % li)
                self.ple(li, s)
        S.fence()
        if self.dbg == 6:
            for c in range(8):
                S.dma("sp", "dbg", self.out[s, c * 128:(c + 1) * 128, :], self.hT[:, c, 0:1024], reads=[self.hB[c][0], self.hB[c][1]])
            S.fence()
            return
        if self.dbg == 3:
            rt, rtb = self.rstd_tile(0, self.hT, self.hB, 8)
            S.dma("sp", "dbg", self.out[s, 0:128, 0:512], rt[:], reads=[rtb])
            return
        self.final(s)
        S.fence()

    def evac(self, dst, src, reads, writes):
        S = self.S
        if self.evac_eng() == "act":
            S.op("act", lambda E: E.activation(out=dst, in_=src, func=AF.Copy), reads, writes)
        else:
            S.op("dve", lambda E: E.tensor_copy(out=dst, in_=src), reads, writes)

    def load_w2(self, srcs, shape):
        i = self.ring_i
        self.ring_i = (i + 1) % self.NSLOT
        a, b = shape
        assert a * b <= 4096
        view = self.ring[i][:, 0:a * b].rearrange("p (a b) -> p a b", a=a)
        o = 0
        for src, nb in srcs:
            self.S.dma("pool", "ring%d" % i, view[:, :, o:o + nb], src, writes=[self.ringB[i]], batch=(o > 0))
            o += nb
        return view, self.ringB[i]

    def proj_fm(self, w, wb, m, tt, kcs=8, src=None, srcB=None):
        S = self.S
        src = self.A if src is None else src
        ps, pb = self.bank()
        lo = (srcB is None and tt == 0)
        for kc in range(kcs):
            rb = self.AB[kc][tt] if srcB is None else srcB
            last = (kc == kcs - 1) and not lo
            S.op("pe", lambda E, ps=ps, kc=kc, last=last: E.matmul(
                ps, w[:, kc, m * 128:(m + 1) * 128], src[:, kc, tt * 512:(tt + 1) * 512],
                start=(kc == 0), stop=last), [wb, rb], [pb], inc=last)
        if lo:
            for kc in range(kcs):
                S.op("pe", lambda E, ps=ps, kc=kc: E.matmul(
                    ps[:, 0:256], w[:, kc, m * 128:(m + 1) * 128], self.Alo[:, kc, :], start=False, stop=(kc == kcs - 1)),
                    [wb, self.AloB[kc]], [pb], inc=(kc == kcs - 1))
        return ps, pb

    def proj_tm(self, w, wb, blk, ncols):
        S = self.S
        ps, pb = self.bank()
        lo = blk < 2
        for kc in range(8):
            last = (kc == 7) and not lo
            S.op("pe", lambda E, ps=ps, kc=kc, last=last: E.matmul(
                ps[:, 0:ncols], self.A[:, kc, blk * 128:(blk + 1) * 128], w[:, kc, 0:ncols],
                start=(kc == 0), stop=last), [wb, self.AB[kc][blk // 4]], [pb], inc=last)
        if lo:
            for kc in range(8):
                S.op("pe", lambda E, ps=ps, kc=kc: E.matmul(
                    ps[:, 0:ncols], self.Alo[:, kc, blk * 128:(blk + 1) * 128], w[:, kc, 0:ncols],
                    start=False, stop=(kc == 7)), [wb, self.AloB[kc]], [pb], inc=(kc == 7))
        return ps[:, 0:ncols], pb

    def out_proj(self, Wrows, kcs, srcT, srcB):
        S = self.S
        for half in range(2):
            w, wb = self.load_w(Wrows.rearrange("(kc p) n -> p kc n", p=128)[:, :, half * 512:(half + 1) * 512],
                                (kcs, 512))
            for tt in range(4):
                for m in range(4):
                    oc = half * 4 + m
                    ps, pb = self.bank()
                    for kc in range(kcs):
                        S.op("pe", lambda E, ps=ps, kc=kc, m=m, tt=tt, w=w: E.matmul(
                            ps, w[:, kc, m * 128:(m + 1) * 128], srcT[:, kc, tt * 512:(tt + 1) * 512],
                            start=(kc == 0), stop=(kc == kcs - 1)), [wb, srcB], [pb], inc=(kc == kcs - 1))
                    self.add_to_h(ps, pb, oc, tt)

    def fox(self, j, s):
        S = self.S
        Win = self.w["fox_w_in"][j]
        Wv3 = Win.rearrange("(kc p) n -> p kc n", p=128)
        qT = self.bv(0, [2, T]); kT = self.bv(4096, [2, T]); v = self.bv(8192, [16, 256]); oT = self.bv(12288, [2, T])
        Fc = self.bv(16384, [T], F32, (0, 16)); lf = self.bv(20480, [T], F32, (0, 16))
        nFt = self.bv(24576, [256], F32)
        masks = self.bv(25088, [4, 512])
        Fsplit = [self.bv(20480 + i * 1536, [3, 512], BF16, (0, 16)) for i in range(2)]
        onesf = self.bv(27136, [128], F32, (0, 16))
        Fm = [self.bv(27392 + i * 1024, [512], F32, (0, 16)) for i in range(2)]
        ones16 = self.bv(29440, [512], F32, (0, 16))
        nbf = self.bv(30464, [1], F32, (0, 16))
        fqT = self.bv(30720, [512], F32)
        fqTB = Buf()
        qB, kB, vB, oB, FB, lfB, nFtB, mB, cB2 = Buf(), Buf(), Buf(), Buf(), Buf(), Buf(), Buf(), Buf(), Buf()
        FmB = [Buf(), Buf()]
        S.op("pool", lambda E: E.memset(onesf, 1.0), [], [cB2])
        S.op("pool", lambda E: E.memset(ones16, 1.0), [], [cB2])
        S.dma("sp", "misc", nbf, self.w["fox_b_f"][j].rearrange("(h o) -> h o", o=1), writes=[cB2])
        S.op("dve", lambda E: E.tensor_scalar(out=nbf, in0=nbf, scalar1=-1.0, scalar2=None, op0=ALU.mult), [cB2], [cB2])
        for r in range(4):
            t32, tb = self.tmp32()
            S.op("pool", lambda E, t32=t32: E.memset(t32[:], 1.0), [], [tb])
            S.op("pool", lambda E, t32=t32, r=r: E.affine_select(
                out=t32[:], in_=t32[:], pattern=[[1, 512]], compare_op=ALU.is_ge, fill=0.0, base=-128 * r,
                channel_multiplier=-1), [tb], [tb])
            S.op("dve", lambda E, t32=t32, r=r: E.tensor_copy(out=masks[:, r, :], in_=t32[:]), [tb], [mB])
        wf, wfb = self.load_w(Wv3[:, :, 3072:3088], (8, 16))
        for tt in range(4):
            ps, pb = self.bank()
            for kc in range(8):
                S.op("pe", lambda E, ps=ps, kc=kc, tt=tt: E.matmul(
                    ps[0:16, :], wf[:, kc, 0:16], self.A[:, kc, tt * 512:(tt + 1) * 512],
                    start=(kc == 0), stop=(kc == 7)), [wfb, self.AB[kc][tt]], [pb])
            sl = slice(tt * 512, (tt + 1) * 512)
            S.op("act", lambda E, ps=ps, sl=sl: E.activation(out=lf[:, sl], in_=ps[0:16, :], func=AF.Exp, bias=nbf,
                                                             scale=-1.0), [pb, cB2], [lfB])
            S.op("act", lambda E, sl=sl: E.activation(out=lf[:, sl], in_=lf[:, sl], func=AF.Ln, bias=1.0, scale=1.0),
                 [lfB], [lfB])
            init = 0.0 if tt == 0 else Fc[:, tt * 512 - 1:tt * 512]
            S.op("dve", lambda E, sl=sl, init=init: E.tensor_tensor_scan(
                out=Fc[:, sl], data0=ones16, data1=lf[:, sl], initial=init, op0=ALU.mult, op1=ALU.subtract),
                [lfB, cB2, FB], [FB])
        ps, pb = self.bank()
        for blk in range(16):
            S.op("pe", lambda E, ps=ps, blk=blk: E.transpose(
                out=ps[:, blk * 16:(blk + 1) * 16], in_=Fc[:, blk * 128:(blk + 1) * 128],
                identity=self.ident[0:16, 0:16]), [FB, self.cB], [pb])
        S.op("dve", lambda E, ps=ps: E.tensor_scalar(out=nFt, in0=ps[:, 0:256], scalar1=-1.0, scalar2=None,
                                                     op0=ALU.mult), [pb], [nFtB])
        for qp in range(4):
            S.fence()
            wqk, wqkb = self.load_w2([(Wv3[:, :, 256 * qp:256 * qp + 256], 256),
                                      (Wv3[:, :, 1024 + 256 * qp:1024 + 256 * qp + 256], 256)], (8, 512))
            wv, wvb = self.load_w(Wv3[:, :, 2048 + 256 * qp:2048 + 256 * qp + 256], (8, 256))
            for tt in range(4):
                for m in range(4):
                    ps, pb = self.proj_fm(wqk, wqkb, m, tt)
                    dst = (qT if m < 2 else kT)[:, m % 2, tt * 512:(tt + 1) * 512]
                    self.evac(dst, ps, [pb], [qB if m < 2 else kB])
            for blk in range(16):
                ps, pb = self.proj_tm(wv, wvb, blk, 256)
                self.evac(v[:, blk, :], ps, [pb], [vB])
            for hl in range(4):
                h = 4 * qp + hl
                ch, base = hl // 2, 64 * (hl % 2)
                for tt in range(4):
                    fm, fmb = Fm[tt % 2], FmB[tt % 2]
                    S.op("dve", lambda E, fm=fm, tt=tt, h=h: E.tensor_scalar(
                        out=fm, in0=Fc[:, tt * 512:(tt + 1) * 512], scalar1=self.ident[0:16, h:h + 1], scalar2=None,
                        op0=ALU.mult), [FB, self.cB], [fmb])
                    fs = Fsplit[tt % 2]
                    S.op("dve", lambda E, fs=fs, fm=fm: E.tensor_copy(out=fs[:, 0, :], in_=fm), [fmb], [fmb])
                    S.op("dve", lambda E, fs=fs, fm=fm: E.tensor_tensor(out=fm, in0=fm, in1=fs[:, 0, :], op=ALU.subtract),
                         [fmb], [fmb])
                    S.op("dve", lambda E, fs=fs, fm=fm: E.tensor_copy(out=fs[:, 1, :], in_=fm), [fmb], [fmb])
                    S.op("dve", lambda E, fs=fs, fm=fm: E.tensor_tensor(out=fm, in0=fm, in1=fs[:, 1, :], op=ALU.subtract),
                         [fmb], [fmb])
                    S.op("dve", lambda E, fs=fs, fm=fm: E.tensor_copy(out=fs[:, 2, :], in_=fm), [fmb], [fmb])
                    ps, pb = self.bank()
                    for k3 in range(3):
                        S.op("pe", lambda E, ps=ps, fs=fs, k3=k3: E.matmul(ps, self.onesb[0:16, :], fs[:, k3, :],
                                                                          start=(k3 == 0), stop=(k3 == 2)),
                             [fmb, self.cB], [pb])
                    fq, fqb = fqT, fqTB
                    S.op("act", lambda E, fq=fq, ps=ps: E.activation(out=fq[:], in_=ps, func=AF.Copy), [pb], [fqb])
                    Ops, Ob = self.dbank(0)
                    Dps, Db = self.dbank(1)
                    nkb = 4 * tt + 4
                    for kb in range(nkb):
                        ps, pb = self.bank()
                        S.op("pe", lambda E, ps=ps, kb=kb, tt=tt, ch=ch, base=base: E.matmul(
                            ps, kT[base:base + 64, ch, kb * 128:(kb + 1) * 128],
                            qT[base:base + 64, ch, tt * 512:(tt + 1) * 512], start=True, stop=True), [kB, qB], [pb])
                        sp, spb = self.tmp32()
                        S.op("dve", lambda E, sp=sp, ps=ps, fq=fq: E.scalar_tensor_tensor(
                            out=sp[:], in0=ps, scalar=0.125, in1=fq[:], op0=ALU.mult, op1=ALU.add), [pb, fqb], [spb])
                        i = self.sq_i
                        self.sq_i ^= 1
                        pt, ptb = self.sq[i], self.sqB[i]
                        r = kb - 4 * tt
                        if r >= 0:
                            S.op("dve", lambda E, sp=sp, kb=kb, h=h: E.tensor_scalar(
                                out=sp[:], in0=sp[:], scalar1=nFt[:, kb * 16 + h:kb * 16 + h + 1], scalar2=30.0,
                                op0=ALU.add, op1=ALU.min), [spb, nFtB], [spb])
                            S.op("act", lambda E, pt=pt, sp=sp: E.activation(out=pt[:], in_=sp[:], func=AF.Exp),
                                 [spb], [ptb])
                        else:
                            S.op("act", lambda E, pt=pt, sp=sp, kb=kb, h=h: E.activation(
                                out=pt[:], in_=sp[:], func=AF.Exp, bias=nFt[:, kb * 16 + h:kb * 16 + h + 1],
                                scale=1.0), [spb, nFtB], [ptb])
                        if r >= 0:
                            S.op("pool", lambda E, pt=pt, r=r: E.tensor_tensor(out=pt[:], in0=pt[:], in1=masks[:, r, :],
                                                                             op=ALU.mult), [ptb, mB], [ptb])
                        S.op("pe", lambda E, pt=pt, kb=kb, hl=hl, base=base, Ops=Ops, nkb=nkb: E.matmul(
                            Ops[base:base + 64, :], v[:, kb, hl * 64:(hl + 1) * 64], pt[:], start=(kb == 0),
                            stop=(kb == nkb - 1)), [vB, ptb], [Ob])
                        S.op("pe", lambda E, pt=pt, kb=kb, base=base, Dps=Dps, nkb=nkb: E.matmul(
                            Dps[base:base + 64, :], self.onesb[:, 0:64], pt[:], start=(kb == 0),
                            stop=(kb == nkb - 1)), [self.cB, ptb], [Db])
                    rd, rdb = self.tmp32()
                    S.op("dve", lambda E, rd=rd, Dps=Dps, base=base: E.reciprocal(
                        out=rd[base:base + 64, :], in_=Dps[base:base + 64, :]), [Db], [rdb])
                    S.op("dve", lambda E, rd=rd, Ops=Ops, base=base, ch=ch, tt=tt: E.tensor_tensor(
                        out=oT[base:base + 64, ch, tt * 512:(tt + 1) * 512], in0=Ops[base:base + 64, :],
                        in1=rd[base:base + 64, :], op=ALU.mult), [Ob, rdb], [oB])
            self.out_proj(self.w["fox_w_out"][j][256 * qp:256 * qp + 256, :], 2, oT, oB)

    def gla(self, j, s):
        S = self.S
        Win = self.w["gla_w_in"][j]
        W3 = Win.rearrange("(kc p) n -> p kc n", p=128)
        qd = self.bv(0, [T]); kupdT = self.bv(2048, [T])
        kinvb = self.bv(4096, [T])
        bcum = self.bv(8192, [T], F32); kinv = bcum
        Eb = self.bv(12288, [T], F32)
        v = self.bv(16384, [16, 256]); ktok = self.bv(20480, [16, 128]); oT = self.bv(22528, [2, T])
        glrT = self.bv(26624, [T], BF16, (0, 16))
        state = self.bv(28672, [256], F32); statebf = self.bv(29184, [256])
        ebl = self.bv(29440, [16], F32); rmask = self.bv(29472, [512], F32)
        cmask = self.bv(30496, [128]); attT = [self.bv(30624 + i * 128, [128]) for i in range(2)]
        wg2 = self.bv(30880, [512], BF16, (0, 16)); nbg = self.bv(31392, [4], F32)
        (qB, kiB, kuB, bcB, eB, vB, ktB, oB, glB, stB, sbB, eblB, cB2) = [Buf() for _ in range(13)]
        attB = [Buf(), Buf()]
        kibB = Buf()
        S.dma("pool", "misc2", wg2, self.w["gla_w_g2"][j], writes=[cB2])
        S.dma("sp", "misc", nbg, self.w["gla_b_g"][j].rearrange("(c p) -> p c", p=128), writes=[cB2], slow=True)
        S.op("dve", lambda E: E.tensor_scalar(out=nbg, in0=nbg, scalar1=-1.0, scalar2=None, op0=ALU.mult), [cB2], [cB2])
        S.op("pool", lambda E: E.memset(rmask, 1.0), [], [cB2])
        for i in range(4):
            S.op("pool", lambda E, i=i: E.memset(rmask[:, 128 * i:128 * i + 1], 0.0), [cB2], [cB2])
        t32, tb = self.tmp32()
        S.op("pool", lambda E: E.memset(t32[:, 0:128], 1.0), [], [tb])
        S.op("pool", lambda E: E.affine_select(out=t32[:, 0:128], in_=t32[:, 0:128], pattern=[[1, 128]],
                                               compare_op=ALU.is_ge, fill=0.0, base=0, channel_multiplier=-1), [tb], [tb])
        S.op("dve", lambda E: E.tensor_copy(out=cmask, in_=t32[:, 0:128]), [tb], [cB2])
        wl, wlb = self.load_w(W3[:, :, 3072:3088], (8, 16))
        for tt in range(4):
            ps, pb = self.bank()
            for kc in range(8):
                S.op("pe", lambda E, ps=ps, kc=kc, tt=tt: E.matmul(
                    ps[0:16, :], wl[:, kc, 0:16], self.A[:, kc, tt * 512:(tt + 1) * 512],
                    start=(kc == 0), stop=(kc == 7)), [wlb, self.AB[kc][tt]], [pb])
            self.evac(glrT[:, tt * 512:(tt + 1) * 512], ps[0:16, :], [pb], [glB])
        for h in range(4):
            S.fence()
            for tt in range(4):
                sl = slice(tt * 512, (tt + 1) * 512)
                ps, pb = self.bank()
                S.op("pe", lambda E, ps=ps, sl=sl, h=h: E.matmul(ps, wg2[:, h * 128:(h + 1) * 128], glrT[:, sl],
                                                                 start=True, stop=True), [cB2, glB], [pb])
                t1, t1b = self.tmp32()
                S.op("act", lambda E, t1=t1, ps=ps, h=h: E.activation(out=t1[:], in_=ps, func=AF.Exp,
                                                                      bias=nbg[:, h:h + 1], scale=-1.0), [pb, cB2], [t1b])
                S.op("act", lambda E, t1=t1: E.activation(out=t1[:], in_=t1[:], func=AF.Ln, bias=1.0, scale=1.0),
                     [t1b], [t1b])
                S.op("dve", lambda E, t1=t1: E.tensor_scalar(out=t1[:], in0=t1[:], scalar1=-1.0 / 16.0, scalar2=None,
                                                             op0=ALU.mult), [t1b], [t1b])
                S.op("dve", lambda E, t1=t1, sl=sl: E.tensor_tensor_scan(
                    out=bcum[:, sl], data0=rmask, data1=t1[:], initial=0.0, op0=ALU.mult, op1=ALU.add),
                    [t1b, cB2], [bcB])
                S.op("act", lambda E, sl=sl: E.activation(out=Eb[:, sl], in_=bcum[:, sl], func=AF.Exp), [bcB], [eB])
            S.op("act", lambda E: E.activation(out=ebl, in_=bcum.rearrange("p (c t) -> p c t", t=128)[:, :, 127],
                                               func=AF.Exp), [bcB], [eblB])
            wqk, wqkb = self.load_w2([(W3[:, :, 128 * h:128 * h + 128], 128),
                                      (W3[:, :, 512 + 128 * h:512 + 128 * h + 128], 128)], (8, 256))
            wv, wvb = self.load_w(W3[:, :, 1024 + 256 * h:1024 + 256 * h + 256], (8, 256))
            for tt in range(4):
                sl = slice(tt * 512, (tt + 1) * 512)
                ps, pb = self.proj_fm(wqk, wqkb, 0, tt)
                S.op("dve", lambda E, ps=ps, sl=sl: E.scalar_tensor_tensor(
                    out=qd[:, sl], in0=ps, scalar=128.0 ** -0.5, in1=Eb[:, sl], op0=ALU.mult, op1=ALU.mult),
                    [pb, eB], [qB])
            for tt in range(4):
                sl = slice(tt * 512, (tt + 1) * 512)
                S.op("act", lambda E, sl=sl: E.activation(out=Eb[:, sl], in_=bcum[:, sl], func=AF.Exp, scale=-1.0),
                     [bcB, eB], [eB])
                ps, pb = self.proj_fm(wqk, wqkb, 1, tt)
                S.op("dve", lambda E, ps=ps, sl=sl: E.tensor_tensor(out=kinv[:, sl], in0=ps, in1=Eb[:, sl], op=ALU.mult),
                     [pb, eB, bcB], [kiB, bcB])
                S.op("act", lambda E, sl=sl: E.activation(out=kinvb[:, sl], in_=kinv[:, sl], func=AF.Copy), [kiB], [kibB])
                S.op("dve", lambda E, sl=sl, tt=tt: E.tensor_tensor(
                    out=kupdT[:, sl].rearrange("p (c t) -> p c t", t=128),
                    in0=kinv[:, sl].rearrange("p (c t) -> p c t", t=128),
                    in1=ebl[:, 4 * tt:4 * tt + 4].unsqueeze(2).to_broadcast([128, 4, 128]), op=ALU.mult),
                    [kiB, eblB], [kuB])
            for g in range(2):
                ps, pb = self.tbank()
                for q in range(8):
                    blk = g * 8 + q
                    S.op("pe", lambda E, ps=ps, q=q, blk=blk: E.transpose(
                        out=ps[:, q * 128:(q + 1) * 128], in_=kupdT[:, blk * 128:(blk + 1) * 128],
                        identity=self.identb[:]), [kuB, self.cB], [pb])
                self.evac(ktok[:, g * 8:(g + 1) * 8, :], ps.rearrange("p (q t) -> p q t", q=8), [pb], [ktB])
            for blk in range(16):
                ps, pb = self.proj_tm(wv, wvb, blk, 256)
                self.evac(v[:, blk, :], ps, [pb], [vB])
            S.op("pool", lambda E: E.memset(state, 0.0), [stB], [stB])
            S.op("pool", lambda E: E.memset(statebf, 0.0), [sbB], [sbB])
            for c in range(16):
                cs = slice(c * 128, (c + 1) * 128)
                ps, pb = self.bank()
                S.op("pe", lambda E, ps=ps, cs=cs: E.matmul(ps[:, 0:128], kinvb[:, cs], qd[:, cs], start=True,
                                                            stop=True), [kibB, qB], [pb])
                at, atb = attT[c % 2], attB[c % 2]
                S.op("dve", lambda E, ps=ps, at=at: E.tensor_tensor(out=at, in0=ps[:, 0:128], in1=cmask, op=ALU.mult),
                     [pb, cB2], [atb])
                po, pob = self.bank()
                for m in range(2):
                    S.op("pe", lambda E, po=po, m=m, c=c, at=at: E.matmul(
                        po[:, m * 128:(m + 1) * 128], v[:, c, m * 128:(m + 1) * 128], at, start=True, stop=False),
                        [vB, atb], [pob])
                    S.op("pe", lambda E, po=po, m=m, cs=cs: E.matmul(
                        po[:, m * 128:(m + 1) * 128], statebf[:, m * 128:(m + 1) * 128], qd[:, cs], start=False,
                        stop=True), [sbB, qB], [pob])
                self.evac(oT[:, :, cs], po[:, 0:256].rearrange("p (m t) -> p m t", m=2), [pob], [oB])
                pst, pstb = self.bank()
                S.op("pe", lambda E, pst=pst, c=c: E.matmul(pst[:, 0:256], ktok[:, c, :], v[:, c, :], start=True,
                                                           stop=True), [ktB, vB], [pstb])
                S.op("dve", lambda E, pst=pst, c=c: E.scalar_tensor_tensor(
                    out=state, in0=state, scalar=ebl[:, c:c + 1], in1=pst[:, 0:256], op0=ALU.mult, op1=ALU.add),
                    [pstb, stB, eblB], [stB])
                S.op("act", lambda E: E.activation(out=statebf, in_=state, func=AF.Copy), [stB], [sbB])
            wr, wrb = self.load_w(W3[:, :, 2048 + 256 * h:2048 + 256 * h + 256], (8, 256))
            oBl = [[oB] * 4, [oB] * 4]
            for tt in range(4):
                sl = slice(tt * 512, (tt + 1) * 512)
                rt, rtb = self.rstd_tile(tt, oT, oBl, 2, scale=1.0 / 256)
                for m in range(2):
                    ps, pb = self.proj_fm(wr, wrb, m, tt)
                    sr, srb = self.tmp32()
                    S.op("act", lambda E, sr=sr, ps=ps: E.activation(out=sr[:], in_=ps, func=AF.Silu), [pb], [srb])
                    t1, t1b = self.tmp32()
                    S.op("dve", lambda E, t1=t1, m=m, sl=sl, rt=rt, h=h: E.scalar_tensor_tensor(
                        out=t1[:], in0=oT[:, m, sl], scalar=self.gain("gla_norm", 2 * h + m), in1=rt[:],
                        op0=ALU.mult, op1=ALU.mult), [oB, rtb, self.cB], [t1b])
                    S.op("dve", lambda E, t1=t1, sr=sr, m=m, sl=sl: E.tensor_tensor(
                        out=oT[:, m, sl], in0=t1[:], in1=sr[:], op=ALU.mult), [t1b, srb, oB], [oB])
            self.out_proj(self.w["gla_w_out"][j][256 * h:256 * h + 256, :], 2, oT, oB)

    def av(self, off, shape, dt=F32, parts=(0, 128)):
        n = int(np.prod(shape))
        mul = 2 if dt != BF16 else 1
        ap = self.A[parts[0]:parts[1]].rearrange("p c n -> p (c n)")[:, off:off + n * mul]
        if dt != BF16:
            ap = ap.bitcast(dt)
        if len(shape) == 1:
            return ap
        names = " ".join("d%d" % i for i in range(len(shape)))
        kw = {"d%d" % i: shape[i] for i in range(len(shape) - 1)}
        return ap.rearrange("p (%s) -> p %s" % (names, names), **kw)

    def s5_views(self):
        Bm = self.bv(24576, [8, 4, 2, 64])
        Cm = self.bv(28672, [8, 4, 2, 64])
        return Bm, Cm

    def s5_coefs(self, P, W, off, lr, li, ldt, pB, want_coef):
        S = self.S
        parts = (0, P)
        n = W * 2
        names = ["dt", "mag", "th", "t1", "t2", "sin", "cos", "ar", "ai", "den", "cre", "cim", "nr"]
        V = {k: self.av(off + i * n, [W], F32, parts) for i, k in enumerate(names)}
        ti = self.av(off + len(names) * n, [W], mybir.dt.int32, parts)
        R, Wr = [pB], [pB]
        TWO_PI = 2.0 * math.pi

        def dve(fn):
            S.op("dve", fn, R, Wr)

        def act(fn):
            S.op("act", fn, R, Wr)

        act(lambda E: E.activation(out=V["dt"], in_=ldt, func=AF.Exp))
        dve(lambda E: E.tensor_tensor(out=V["mag"], in0=lr, in1=V["dt"], op=ALU.mult))
        act(lambda E: E.activation(out=V["mag"], in_=V["mag"], func=AF.Exp))
        dve(lambda E: E.tensor_tensor(out=V["th"], in0=li, in1=V["dt"], op=ALU.mult))
        for name, shift in (("sin", 0.0), ("cos", 0.5 * math.pi)):
            o = V[name]
            dve(lambda E, shift=shift: E.tensor_scalar(out=V["t2"], in0=V["th"], scalar1=shift, scalar2=None,
                                                        op0=ALU.add))
            dve(lambda E: E.tensor_scalar(out=V["t1"], in0=V["t2"], scalar1=1.0 / TWO_PI, scalar2=None, op0=ALU.mult))
            dve(lambda E: E.tensor_copy(out=ti, in_=V["t1"]))
            dve(lambda E: E.tensor_copy(out=V["t1"], in_=ti))
            dve(lambda E: E.scalar_tensor_tensor(out=V["t1"], in0=V["t1"], scalar=-TWO_PI, in1=V["t2"], op0=ALU.mult,
                                                 op1=ALU.add))
            dve(lambda E: E.tensor_scalar(out=V["t2"], in0=V["t1"], scalar1=math.pi, scalar2=-TWO_PI, op0=ALU.is_gt,
                                          op1=ALU.mult))
            dve(lambda E: E.tensor_tensor(out=V["t1"], in0=V["t1"], in1=V["t2"], op=ALU.add))
            dve(lambda E: E.tensor_scalar(out=V["t2"], in0=V["t1"], scalar1=-math.pi, scalar2=TWO_PI, op0=ALU.is_lt,
                                          op1=ALU.mult))
            dve(lambda E: E.tensor_tensor(out=V["t1"], in0=V["t1"], in1=V["t2"], op=ALU.add))
            act(lambda E, o=o: E.activation(out=o, in_=V["t1"], func=AF.Sin))
        dve(lambda E: E.tensor_tensor(out=V["ar"], in0=V["mag"], in1=V["cos"], op=ALU.mult))
        dve(lambda E: E.tensor_tensor(out=V["ai"], in0=V["mag"], in1=V["sin"], op=ALU.mult))
        if want_coef:
            dve(lambda E: E.tensor_tensor(out=V["den"], in0=lr, in1=lr, op=ALU.mult))
            dve(lambda E: E.tensor_tensor(out=V["t1"], in0=li, in1=li, op=ALU.mult))
            dve(lambda E: E.tensor_tensor(out=V["den"], in0=V["den"], in1=V["t1"], op=ALU.add))
            dve(lambda E: E.reciprocal(out=V["t2"], in_=V["den"]))
            dve(lambda E: E.tensor_scalar(out=V["nr"], in0=V["ar"], scalar1=-1.0, scalar2=None, op0=ALU.add))
            dve(lambda E: E.tensor_tensor(out=V["cre"], in0=V["nr"], in1=lr, op=ALU.mult))
            dve(lambda E: E.tensor_tensor(out=V["t1"], in0=V["ai"], in1=li, op=ALU.mult))
            dve(lambda E: E.tensor_tensor(out=V["cre"], in0=V["cre"], in1=V["t1"], op=ALU.add))
            dve(lambda E: E.tensor_tensor(out=V["cre"], in0=V["cre"], in1=V["t2"], op=ALU.mult))
            dve(lambda E: E.tensor_tensor(out=V["cim"], in0=V["ai"], in1=lr, op=ALU.mult))
            dve(lambda E: E.tensor_tensor(out=V["t1"], in0=V["nr"], in1=li, op=ALU.mult))
            dve(lambda E: E.tensor_tensor(out=V["cim"], in0=V["cim"], in1=V["t1"], op=ALU.subtract))
            dve(lambda E: E.tensor_tensor(out=V["cim"], in0=V["cim"], in1=V["t2"], op=ALU.mult))
        return V

    def s5_prologue(self, j):
        S = self.S
        S.fence()
        pB = Buf("s5pro")
        Bm, Cm = self.s5_views()
        BmB, CmB = self.s5B
        w = self.w
        lr = self.av(0, [32], F32); li = self.av(64, [32], F32); ldt = self.av(128, [32], F32)
        for m in range(2):
            for src, dst in ((w["s5_lam_re"][j], lr), (w["s5_lam_im"][j], li)):
                for q in range(4):
                    S.dma("sp", "misc", dst[64 * m:64 * m + 64, :].rearrange("p (a b) -> p a b", a=8)[:, :, q],
                          src.rearrange("(cc m q) n -> m q n cc", m=2, q=4)[m, q], writes=[pB], slow=True)
            S.dma("sp", "misc", ldt[64 * m:64 * m + 64, :].rearrange("p (a b) -> p a b", a=8),
                  w["s5_log_dt"][j].rearrange("(cc m q) -> m cc q", m=2, q=4)[m].partition_broadcast(64),
                  writes=[pB], slow=True)
        V = self.s5_coefs(128, 32, 192, lr, li, ldt, pB, False)
        Ac = self.Acoef
        acB = self.acB
        S.op("dve", lambda E: E.tensor_copy(out=Ac[:, 0, 0, :], in_=V["ar"]), [pB], [acB])
        S.op("dve", lambda E: E.tensor_copy(out=Ac[:, 1, 1, :], in_=V["ar"]), [pB], [acB])
        S.op("dve", lambda E: E.tensor_copy(out=Ac[:, 1, 0, :], in_=V["ai"]), [pB], [acB])
        S.op("dve", lambda E: E.tensor_scalar(out=Ac[:, 0, 1, :], in0=V["ai"], scalar1=-1.0, scalar2=None,
                                              op0=ALU.mult), [pB], [acB])
        rm8 = self.av(1600, [8], F32)
        S.op("pool", lambda E: E.memset(rm8, 1.0), [pB], [pB])
        S.op("pool", lambda E: E.affine_select(out=rm8, in_=rm8, pattern=[[-16, 8]], compare_op=ALU.is_ge, fill=0.0,
                                               base=0, channel_multiplier=1), [pB], [pB])
        S.op("pool", lambda E: E.affine_select(out=rm8, in_=rm8, pattern=[[16, 8]], compare_op=ALU.is_ge, fill=0.0,
                                               base=15, channel_multiplier=-1), [pB], [pB])
        rowm = self.rowm
        S.op("dve", lambda E: E.tensor_tensor(out=rowm[:], in0=rm8[:, 0:4], in1=rm8[:, 4:8], op=ALU.add), [pB], [pB])
        colm = self.colm
        S.op("pool", lambda E: E.memset(colm[:], 0.0), [pB], [pB])
        for q in range(4):
            S.op("pool", lambda E, q=q: E.memset(colm[:, q, 16 * q:16 * q + 16], 1.0), [pB], [pB])
        dg = self.diagD
        for cc in range(8):
            S.op("dve", lambda E, cc=cc: E.tensor_scalar(out=dg[:, cc, :], in0=self.ident[:],
                                                         scalar1=self.gain("s5d%d" % j, cc), scalar2=None,
                                                         op0=ALU.mult), [self.cB, pB], [pB])
        Cin = [self.av(4096 + 2048 * r, [8, 2, 64], F32, (0, 64)) for r in range(2)]
        for r, nm in enumerate(("s5_c_re", "s5_c_im")):
            S.dma("sp", "misc", Cin[r], w[nm][j].rearrange("(cc m q) c n -> (q c) cc m n", m=2, q=4), writes=[pB])
        for r in range(2):
            for cc in range(8):
                ps, pb = self.bank()
                S.op("pe", lambda E, ps=ps, r=r, cc=cc: E.transpose(
                    out=ps[:, 0:64], in_=Cin[r][:, cc, :, :].rearrange("p m n -> p (m n)"),
                    identity=self.ident[0:64, 0:64]), [pB, self.cB], [pb])
                for q in range(4):
                    S.op("dve", lambda E, ps=ps, r=r, cc=cc, q=q: E.scalar_tensor_tensor(
                        out=Cm[:, cc, q, r, :], in0=ps[:, 0:64], scalar=(1.0 if r == 0 else -1.0), in1=colm[:, q, :],
                        op0=ALU.mult, op1=ALU.mult), [pb, pB], [CmB])
        S.fence()
        lrn = self.av(0, [64], F32, (0, 64)); lin = self.av(128, [64], F32, (0, 64)); ldn = self.av(256, [64], F32, (0, 64))
        S.dma("sp", "misc", lrn, w["s5_lam_re"][j].rearrange("g n -> n g"), writes=[pB], slow=True)
        S.dma("sp", "misc", lin, w["s5_lam_im"][j].rearrange("g n -> n g"), writes=[pB], slow=True)
        S.dma("sp", "misc", ldn, w["s5_log_dt"][j].partition_broadcast(64), writes=[pB], slow=True)
        Vn = self.s5_coefs(64, 64, 384, lrn, lin, ldn, pB, True)
        bre = self.av(4096, [64, 16], F32, (0, 64)); bim = self.av(6144, [64, 16], F32, (0, 64))
        bbr = self.av(8192, [64, 16], F32, (0, 64)); bbi = self.av(10240, [64, 16], F32, (0, 64))
        tmpb = self.av(12288, [64, 16], F32, (0, 64))
        S.dma("sp", "misc", bre, w["s5_b_re"][j].rearrange("g n c -> n g c"), writes=[pB])
        S.dma("sp", "misc", bim, w["s5_b_im"][j].rearrange("g n c -> n g c"), writes=[pB])
        cre_b = Vn["cre"].unsqueeze(2).to_broadcast([64, 64, 16])
        cim_b = Vn["cim"].unsqueeze(2).to_broadcast([64, 64, 16])
        S.op("dve", lambda E: E.tensor_tensor(out=bbr, in0=bre, in1=cre_b, op=ALU.mult), [pB], [pB])
        S.op("dve", lambda E: E.tensor_tensor(out=tmpb, in0=bim, in1=cim_b, op=ALU.mult), [pB], [pB])
        S.op("dve", lambda E: E.tensor_tensor(out=bbr, in0=bbr, in1=tmpb, op=ALU.subtract), [pB], [pB])
        S.op("dve", lambda E: E.tensor_tensor(out=bbi, in0=bim, in1=cre_b, op=ALU.mult), [pB], [pB])
        S.op("dve", lambda E: E.tensor_tensor(out=tmpb, in0=bre, in1=cim_b, op=ALU.mult), [pB], [pB])
        S.op("dve", lambda E: E.tensor_tensor(out=bbi, in0=bbi, in1=tmpb, op=ALU.add), [pB], [pB])
        for r, bb in enumerate((bbr, bbi)):
            for cc in range(8):
                ps, pb = self.bank()
                S.op("pe", lambda E, ps=ps, bb=bb, cc=cc: E.transpose(
                    out=ps[:, 0:64], in_=bb[:, 8 * cc:8 * cc + 8, :].rearrange("p g c -> p (g c)"),
                    identity=self.ident[0:64, 0:64]), [pB, self.cB], [pb])
                for q in range(4):
                    S.op("dve", lambda E, ps=ps, r=r, cc=cc, q=q: E.tensor_scalar(
                        out=Bm[:, cc, q, r, :], in0=ps[:, 0:64], scalar1=rowm[:, q:q + 1], scalar2=None, op0=ALU.mult),
                        [pb, pB], [BmB])
        S.fence()

    def s5(self, j, s):
        S = self.S
        NT = 32
        NCH = T // NT
        Bm, Cm = self.s5_views()
        BmB, CmB = self.s5B
        uT = self.bv(0, [8, T])
        uB = Buf("uT")
        Xc = [self.bv(16384 + i * 2048, [NT, 2, 32]) for i in range(2)]
        Hc = [self.bv(20480 + i * 2048, [NT, 2, 32]) for i in range(2)]
        XB = [Buf(), Buf()]
        HB = [Buf(), Buf()]
        ttB, u2B = Buf(), Buf()
        hsB = [Buf(), Buf()]
        Ac, Tt, U2, Hs = self.Acoef, self.Tt, self.U2, self.Hs
        Win = self.w["s5_w_in"][j]
        for half in range(2):
            w, wb = self.load_w(self.wview(Win, half * 512, 512), (8, 512))
            for tt in range(4):
                for m in range(4):
                    ps, pb = self.proj_fm(w, wb, m, tt)
                    self.evac(uT[:, half * 4 + m, tt * 512:(tt + 1) * 512], ps, [pb], [uB])
        S.fence()
        yB = Buf("yT")
        tt2B, u22B = [Buf(), Buf()], [Buf(), Buf()]
        hs2B = [[Buf(), Buf()], [Buf(), Buf()]]
        S.op("pool", lambda E: E.memset(Hs[0][:], 0.0), [hs2B[0][0], hs2B[0][1]], [hs2B[0][0], hs2B[0][1]])

        def emit_x(tc):
            b = tc % 2
            t0 = tc * NT
            for g4 in range(8):
                ps, pb = self.bank()
                for q in range(4):
                    for m in range(2):
                        for r in range(2):
                            col = (q * 2 + r) * NT
                            S.op("pe", lambda E, ps=ps, g4=g4, q=q, m=m, r=r, col=col, t0=t0: E.matmul(
                                ps[64 * m:64 * m + 64, col:col + NT], Bm[64 * m:64 * m + 64, g4, q, r, :],
                                uT[64 * m:64 * m + 64, g4, t0:t0 + NT], start=True, stop=True), [BmB, uB], [pb],
                                inc=(q == 3 and m == 1 and r == 1))
                dst = Xc[b][:, :, :, 4 * g4:4 * g4 + 4].rearrange("p t r q -> p q r t")
                src = ps[:, 0:8 * NT].rearrange("p (q r t) -> p q r t", q=4, r=2)
                S.op("act", lambda E, dst=dst, src=src: E.activation(out=dst, in_=src, func=AF.Copy), [pb], [XB[b]])

        def emit_scan(tc):
            b = tc % 2
            HS = (slice(0, 16), slice(16, 32))
            for i in range(NT):
                t = tc * NT + i
                prev, nxt = Hs[t % 2], Hs[(t + 1) % 2]
                pvB, nxB = hs2B[t % 2], hs2B[(t + 1) % 2]
                for hf in range(2):
                    S.op("dve", lambda E, prev=prev, hf=hf: E.tensor_tensor(
                        out=Tt[:, :, :, HS[hf]], in0=Ac[:, :, :, HS[hf]],
                        in1=prev[:, :, HS[hf]].unsqueeze(1).to_broadcast([128, 2, 2, 16]), op=ALU.mult),
                        [self.acB, pvB[hf]], [tt2B[hf]])
                for hf in range(2):
                    S.op("dve", lambda E, b=b, i=i, hf=hf: E.tensor_tensor(
                        out=U2[:, :, HS[hf]], in0=Tt[:, :, 0, HS[hf]], in1=Xc[b][:, i, :, HS[hf]], op=ALU.add),
                        [tt2B[hf], XB[b]], [u22B[hf]])
                for hf in range(2):
                    S.op("dve", lambda E, nxt=nxt, hf=hf: E.tensor_tensor(
                        out=nxt[:, :, HS[hf]], in0=U2[:, :, HS[hf]], in1=Tt[:, :, 1, HS[hf]], op=ALU.add),
                        [u22B[hf], tt2B[hf]], [nxB[hf]])
                S.op("act", lambda E, nxt=nxt, b=b, i=i: E.activation(out=Hc[b][:, i, :, :], in_=nxt[:], func=AF.Copy),
                     [nxB[0], nxB[1]], [HB[b]])

        def emit_y(tc):
            b = tc % 2
            t0 = tc * NT
            for half in range(2):
                ps, pb = self.bank()
                for c4 in range(4):
                    cc = half * 4 + c4
                    col = c4 * NT
                    S.op("pe", lambda E, ps=ps, cc=cc, col=col, t0=t0: E.matmul(
                        ps[:, col:col + NT], self.diagD[:, cc, :], uT[:, cc, t0:t0 + NT], start=True, stop=False),
                        [self.cB, uB], [pb], inc=False)
                    for m in range(2):
                        for q in range(4):
                            for r in range(2):
                                S.op("pe", lambda E, ps=ps, cc=cc, col=col, m=m, q=q, r=r, b=b: E.matmul(
                                    ps[64 * m:64 * m + 64, col:col + NT], Cm[64 * m:64 * m + 64, cc, q, r, :],
                                    Hc[b][64 * m:64 * m + 64, :, r, 4 * cc + q], start=False,
                                    stop=(q == 3 and r == 1)), [CmB, HB[b]], [pb],
                                    inc=(c4 == 3 and m == 1 and q == 3 and r == 1))
                dst = self.A[:, half * 4:half * 4 + 4, t0:t0 + NT]
                src = ps[:, 0:4 * NT].rearrange("p (c t) -> p c t", c=4)
                S.op("act", lambda E, dst=dst, src=src: E.activation(out=dst, in_=src, func=AF.Gelu_apprx_tanh),
                     [pb], [yB])

        emit_x(0)
        for tc in range(NCH):
            if tc + 1 < NCH:
                emit_x(tc + 1)
            emit_scan(tc)
            emit_y(tc)
        S.fence()
        gT = uT
        gB = Buf("gT")
        Wg3 = self.w["s5_w_glu"][j].rearrange("(kc p) n -> p kc n", p=128)
        for oc in range(8):
            wz, wzb = self.load_w2([(Wg3[:, :, 128 * oc:128 * oc + 128], 128),
                                    (Wg3[:, :, 1024 + 128 * oc:1024 + 128 * oc + 128], 128)], (8, 256))
            for tt in range(4):
                ps1, pb1 = self.proj_fm(wz, wzb, 0, tt, srcB=yB)
                ps2, pb2 = self.proj_fm(wz, wzb, 1, tt, srcB=yB)
                sg, sgb = self.tmp32()
                S.op("act", lambda E, sg=sg, ps2=ps2: E.activation(out=sg[:], in_=ps2, func=AF.Sigmoid), [pb2], [sgb])
                S.op("dve", lambda E, sg=sg, ps1=ps1, oc=oc, tt=tt: E.tensor_tensor(
                    out=gT[:, oc, tt * 512:(tt + 1) * 512], in0=ps1, in1=sg[:], op=ALU.mult), [pb1, sgb], [gB])
        self.out_proj(self.w["s5_w_out"][j], 8, gT, gB)


_CACHE = {}


def get_prog(nseq, **kw):
    key = (nseq, tuple(sorted(kw.items())))
    if key not in _CACHE:
        _CACHE[key] = Prog(nseq, **kw)
    return _CACHE[key]


def kernel(**inputs):
    nseq = 16 // NCORES
    prog = get_prog(nseq)
    x = np.ascontiguousarray(inputs["x"], dtype=np.float32)
    p = np.ascontiguousarray(inputs["p"], dtype=np.float32)
    in_maps = []
    for c in range(NCORES):
        m = {"x": np.ascontiguousarray(x[c * nseq:(c + 1) * nseq]),
             "p": np.ascontiguousarray(p[:, c * nseq:(c + 1) * nseq])}
        for name, shape in WSPEC:
            m[name] = np.ascontiguousarray(inputs[name], dtype=np.float32)
        in_maps.append(m)
    res = run_bass_kernel_spmd(prog.nc, in_maps, core_ids=list(range(NCORES)))
    return np.concatenate([r["out"] for r in res.results], axis=0)
```
